# Optimizing a Trainium2 kernel written in Bass

```python
import jax, jax.numpy as jnp
from jax import lax
import numpy as np

D_MODEL = 2048
BATCH = 4
SEQ = 2048
DEPTH = 4
DEC_BATCH = 128
DEC_SEQ = 4
PAST_LEN = 16384
PAGE_SIZE = 128

N_EVEN = (DEPTH + 1) // 2
N_ODD = DEPTH // 2
A_CHUNK = 128
A_HEADS = 4
A_WIDTH = D_MODEL // 2
A_HEAD_DIM = A_WIDTH // A_HEADS
B_HEADS = 4
B_VAL_WIDTH = D_MODEL // 2
B_KEY_WIDTH = B_VAL_WIDTH // 2
B_DK = B_KEY_WIDTH // B_HEADS
B_DV = B_VAL_WIDTH // B_HEADS
B_GATE_RANK = 16
B_GATE_TAU = 16.0
B_CHUNK = 64
EVEN_IN = 2 * A_WIDTH + 2 * B_KEY_WIDTH + 2 * B_VAL_WIDTH + B_GATE_RANK
EVEN_SPLITS = (A_WIDTH, 2 * A_WIDTH, 2 * A_WIDTH + B_KEY_WIDTH, 2 * A_WIDTH + 2 * B_KEY_WIDTH,
               2 * A_WIDTH + 2 * B_KEY_WIDTH + B_VAL_WIDTH, 2 * A_WIDTH + 2 * B_KEY_WIDTH + 2 * B_VAL_WIDTH)
EVEN_OUT = A_WIDTH + B_VAL_WIDTH
C_WIDTH = D_MODEL
C_CONV = 3
D_FF = 4 * D_MODEL
EPS = 1e-6

kernel_name = "hybrid_chunkmlp_gla_shortconv_decoder_step"


def rmsnorm(x, g):
    x32 = x.astype(jnp.float32)
    y = x32 * lax.rsqrt(jnp.mean(x32 * x32, axis=-1, keepdims=True) + EPS)
    return y.astype(x.dtype) * g.astype(x.dtype)


def chunk_mlp(u, v, w_s, b_s):
    n, t, _ = u.shape
    L = min(A_CHUNK, t)
    nc = t // L
    mask = jnp.tril(jnp.ones((L, L), dtype=bool))
    w = jnp.where(mask, w_s[:, :L, :L], jnp.zeros((), w_s.dtype))
    vr = v.reshape(n, nc, L, A_HEADS, A_HEAD_DIM)
    mixed = jnp.einsum('hts,ncshd->ncthd', w, vr) + jnp.transpose(b_s[:, :L])[None, None, :, :, None]
    return (u.reshape(n, nc, L, A_HEADS, A_HEAD_DIM) * mixed).reshape(n, t, A_WIDTH)


def gla(q, k, v, log_a, s0):
    n, t = q.shape[:2]
    L = min(B_CHUNK, t)
    nc = t // L
    mask = jnp.tril(jnp.ones((L, L), dtype=bool))

    def to_chunks(a):
        return jnp.moveaxis(a.reshape(n, nc, L, *a.shape[2:]), 1, 0)

    def step(S, inp):
        qc, kc, vc, gc = inp
        G = jnp.cumsum(gc, axis=1)
        G_last = G[:, -1]
        q_in = qc * jnp.exp(G)
        k_in = kc * jnp.exp(-G)
        scores = jnp.where(mask, jnp.einsum('nthd,nshd->nhts', q_in, k_in), 0.0)
        o = jnp.einsum('nhts,nshe->nthe', scores, vc) + jnp.einsum('nthd,nhde->nthe', q_in, S)
        k_dec = kc * jnp.exp(G_last[:, None] - G)
        S = S * jnp.exp(G_last)[..., None] + jnp.einsum('nshd,nshe->nhde', k_dec, vc)
        return S, o

    S, o = lax.scan(step, s0, (to_chunks(q), to_chunks(k), to_chunks(v), to_chunks(log_a)))
    o = jnp.moveaxis(o, 0, 1).reshape(n, t, q.shape[2], v.shape[-1])
    return o, S


def even_mixer(h, w_in, w_gate_up, b_gate, w_s, b_s, g_out, w_out, s0):
    n, t, _ = h.shape
    proj = h @ w_in
    u, v, q, k, vb, r, glr = jnp.split(proj, EVEN_SPLITS, axis=-1)
    u = jax.nn.gelu(u)
    v = jax.nn.gelu(v)
    a_out = chunk_mlp(u, v, w_s, b_s)
    f32 = jnp.float32
    log_a = jax.nn.log_sigmoid((glr @ w_gate_up + b_gate).astype(f32)) / B_GATE_TAU
    qh = q.astype(f32).reshape(n, t, B_HEADS, B_DK) * (B_DK ** -0.5)
    kh = k.astype(f32).reshape(n, t, B_HEADS, B_DK)
    vh = vb.astype(f32).reshape(n, t, B_HEADS, B_DV)
    o, S = gla(qh, kh, vh, log_a.reshape(n, t, B_HEADS, B_DK), s0.astype(f32))
    o = rmsnorm(o.astype(h.dtype), g_out.reshape(B_HEADS, B_DV))
    b_out = (o * jax.nn.silu(r.reshape(n, t, B_HEADS, B_DV))).reshape(n, t, B_VAL_WIDTH)
    y = jnp.concatenate([a_out, b_out], axis=-1) @ w_out
    return y, v, S


def conv_mixer(h, w_in, conv_w, w_out, buf):
    t = h.shape[1]
    bg, cg, hx = jnp.split(h @ w_in, 3, axis=-1)
    z = cg * hx
    zp = jnp.concatenate([buf.astype(z.dtype), z], axis=1)
    conv = zp[:, 0:t] * conv_w[0]
    for j in range(1, C_CONV):
        conv = conv + zp[:, j:j + t] * conv_w[j]
    y = (bg * conv) @ w_out
    return y, zp[:, -(C_CONV - 1):]


def trunk(x, gla_states, conv_bufs, norm_mix, norm_ffn, norm_final, w_in_even, w_gate_up, b_gate,
          w_spatial, b_spatial, g_gla_out, w_out_even, w_in_odd, conv_w, w_out_odd, w_ffn_up, w_ffn_down):
    v_rows, new_gla, new_conv = [], [], []
    for l in range(DEPTH):
        i = l // 2
        h = rmsnorm(x, norm_mix[l])
        if l % 2 == 0:
            y, v, S = even_mixer(h, w_in_even[i], w_gate_up[i], b_gate[i], w_spatial[i], b_spatial[i],
                                 g_gla_out[i], w_out_even[i], gla_states[i])
            v_rows.append(v)
            new_gla.append(S.astype(x.dtype))
        else:
            y, nb = conv_mixer(h, w_in_odd[i], conv_w[i], w_out_odd[i], conv_bufs[i])
            new_conv.append(nb)
        x = x + y
        h = rmsnorm(x, norm_ffn[l])
        x = x + jnp.square(jax.nn.relu(h @ w_ffn_up[l])) @ w_ffn_down[l]
    return rmsnorm(x, norm_final), v_rows, new_gla, new_conv


def setup_inputs(seed: int = 0) -> dict:
    key = jax.random.key(seed)
    ks = jax.random.split(key, 20)
    nrm = jax.random.normal
    f32 = jnp.float32
    return {
        "x_prompt": nrm(ks[0], (BATCH, SEQ, D_MODEL), f32),
        "x_sample": nrm(ks[1], (DEC_BATCH, DEC_SEQ, D_MODEL), f32),
        "state_gla": nrm(ks[2], (N_EVEN, DEC_BATCH, B_HEADS, B_DK, B_DV), f32),
        "state_conv": nrm(ks[3], (N_ODD, DEC_BATCH, C_CONV - 1, C_WIDTH), f32),
        "norm_mix": 1.0 + 0.02 * nrm(ks[4], (DEPTH, D_MODEL), f32),
        "norm_ffn": 1.0 + 0.02 * nrm(ks[5], (DEPTH, D_MODEL), f32),
        "norm_final": 1.0 + 0.02 * nrm(ks[6], (D_MODEL,), f32),
        "w_in_even": nrm(ks[7], (N_EVEN, D_MODEL, EVEN_IN), f32) * D_MODEL ** -0.5,
        "w_gate_up": nrm(ks[8], (N_EVEN, B_GATE_RANK, B_KEY_WIDTH), f32) * B_GATE_RANK ** -0.5,
        "b_gate": 0.01 * nrm(ks[9], (N_EVEN, B_KEY_WIDTH), f32),
        "w_spatial": nrm(ks[10], (N_EVEN, A_HEADS, A_CHUNK, A_CHUNK), f32) * A_CHUNK ** -0.5,
        "b_spatial": 1.0 + 0.02 * nrm(ks[11], (N_EVEN, A_HEADS, A_CHUNK), f32),
        "g_gla_out": 1.0 + 0.02 * nrm(ks[12], (N_EVEN, B_VAL_WIDTH), f32),
        "w_out_even": nrm(ks[13], (N_EVEN, EVEN_OUT, D_MODEL), f32) * EVEN_OUT ** -0.5,
        "w_in_odd": nrm(ks[14], (N_ODD, D_MODEL, 3 * C_WIDTH), f32) * D_MODEL ** -0.5,
        "conv_w": nrm(ks[15], (N_ODD, C_CONV, C_WIDTH), f32) * C_CONV ** -0.5,
        "w_out_odd": nrm(ks[16], (N_ODD, C_WIDTH, D_MODEL), f32) * C_WIDTH ** -0.5,
        "w_ffn_up": nrm(ks[17], (DEPTH, D_MODEL, D_FF), f32) * D_MODEL ** -0.5,
        "w_ffn_down": nrm(ks[18], (DEPTH, D_FF, D_MODEL), f32) * (0.5 * D_FF ** -0.5),
    }


def reference(x_prompt, x_sample, state_gla, state_conv, norm_mix, norm_ffn, norm_final, w_in_even,
              w_gate_up, b_gate, w_spatial, b_spatial, g_gla_out, w_out_even, w_in_odd, conv_w,
              w_out_odd, w_ffn_up, w_ffn_down):
    params = (norm_mix, norm_ffn, norm_final, w_in_even, w_gate_up, b_gate, w_spatial, b_spatial,
              g_gla_out, w_out_even, w_in_odd, conv_w, w_out_odd, w_ffn_up, w_ffn_down)
    n_p = x_prompt.shape[0]
    gla0 = [jnp.zeros((n_p, B_HEADS, B_DK, B_DV), jnp.float32) for _ in range(N_EVEN)]
    conv0 = [jnp.zeros((n_p, C_CONV - 1, C_WIDTH), x_prompt.dtype) for _ in range(N_ODD)]
    y_prompt, _, gla_p, conv_p = trunk(x_prompt, gla0, conv0, *params)
    gla_s_in = [state_gla[i] for i in range(N_EVEN)]
    conv_s_in = [state_conv[i] for i in range(N_ODD)]
    y_sample, v_s, gla_s, conv_s = trunk(x_sample, gla_s_in, conv_s_in, *params)
    gla_state_prompt = jnp.stack(gla_p)
    gla_state_sample = jnp.stack(gla_s)
    conv_state_prompt = jnp.stack(conv_p)
    conv_state_sample = jnp.stack(conv_s)
    chunk_v_sample = jnp.stack(v_s)
    return (y_prompt, y_sample, gla_state_prompt, gla_state_sample, conv_state_prompt, conv_state_sample, chunk_v_sample)
```

```python
import contextlib
import numpy as np
import concourse.bass as bass
import concourse.mybir as mybir
from concourse.bass_utils import run_bass_kernel_spmd

F32 = mybir.dt.float32
BF16 = mybir.dt.bfloat16
AF = mybir.ActivationFunctionType
ALU = mybir.AluOpType

NCORES = 8
D = 2048
KC = 16
TP = 512
TS = 64
TT = TP + TS
NPASS = 4
SEQ = 2048
NSEQ_S = 16
DEPTH = 4
EVEN_IN = 5136
DFF = 8192
EPS = 1e-6
WCOLS = 256
NSLOT = 4
SAME_SYNC = True
B_CORES = [0, 1, 4, 5]
A_CORES = [2, 3, 6, 7]

C_ID = 0
C_M128 = 128
C_M64 = 256
C_MS = 384
C_OH = 448
C_RM = 464
NCONST = C_RM + TT


class Sched:
    ENGS = ("pe", "act", "dve", "pool", "sp")

    def __init__(self):
        self.ops = {e: [] for e in self.ENGS}
        self.cnt = {}
        self.res = {}
        self.known = {e: {} for e in self.ENGS}
        self.epoch = 0
        self.dma_cnt = {}
        self.waited = {}
        self.final_waits = {}
        self.dry = False

    def new_epoch(self):
        self.epoch += 1

    def _collect(self, eng, rd, wr):
        need = {}

        def add(evt):
            if evt is None:
                return
            k, idx = evt
            if need.get(k, 0) < idx:
                need[k] = idx
        for r in rd:
            st = self.res.get(r)
            if st:
                add(st[0])
        for w in wr:
            st = self.res.get(w)
            if st:
                add(st[0])
                for k, idx in st[1].items():
                    add((k, idx))
        out = []
        for k, idx in need.items():
            if k[0] == "eng" and k[1] == eng:
                if eng in ("pe", "pool", "sp"):
                    continue
                if not SAME_SYNC:
                    continue
            if self.known[eng].get(k, 0) >= idx:
                continue
            self.known[eng][k] = idx
            out.append((k, idx))
            if k[0] == "eng":
                self.waited.setdefault(k, set()).add(idx)
        return out

    def _update(self, evt, rd, wr):
        k, idx = evt
        for r in rd:
            st = self.res.setdefault(r, [None, {}])
            if st[1].get(k, 0) < idx:
                st[1][k] = idx
        for w in wr:
            self.res[w] = [evt, {}]

    def op(self, eng, fn, rd=(), wr=()):
        if self.dry:
            return
        waits = self._collect(eng, rd, wr)
        k = ("eng", eng, self.epoch)
        n = self.cnt.get(k, 0) + 1
        self.cnt[k] = n
        self.ops[eng].append([fn, waits, (k, n), 1])
        self._update((k, n), rd, wr)

    def fence(self, eng, res):
        if self.dry:
            return
        waits = self._collect(eng, (), res)
        if waits:
            self.ops[eng].append([None, waits, None, 0])

    def dma(self, q, fn, key, rd=(), wr=(), final=False):
        if self.dry:
            return
        waits = self._collect(q, rd, wr)
        k = ("dma", key)
        n = self.dma_cnt.get(k, 0) + 1
        self.dma_cnt[k] = n
        self.ops[q].append([fn, waits, (k, n), 16])
        self._update((k, n), rd, wr)
        if final:
            self.final_waits[k] = n

    def barrier(self, engs=("pe", "act", "dve")):
        if self.dry:
            return
        for e in engs:
            waits = []
            for o in engs:
                if o == e:
                    continue
                for ep in range(self.epoch + 1):
                    k = ("eng", o, ep)
                    n = self.cnt.get(k, 0)
                    if n and self.known[e].get(k, 0) < n:
                        self.known[e][k] = n
                        waits.append((k, n))
                        self.waited.setdefault(k, set()).add(n)
            if waits:
                self.ops[e].append([None, waits, None, 0])

    def emit(self, nc, stack):
        sems = {}
        rank = {}
        for k, s in self.waited.items():
            rank[k] = {idx: i + 1 for i, idx in enumerate(sorted(s))}

        def sem(k):
            if k not in sems:
                sems[k] = stack.enter_context(nc.semaphore("s%d" % len(sems)))
            return sems[k]

        def val(k, idx):
            return 16 * idx if k[0] == "dma" else rank[k][idx]
        fw = [(k, n) for k, n in self.final_waits.items()]
        self.ops["sp"].append([None, fw, None, 0])
        block = stack.enter_context(nc.Block())
        getters = {"pe": block.tensor, "act": block.scalar, "dve": block.vector,
                   "pool": block.gpsimd, "sp": block.sync}
        for e in self.ENGS:
            ops = self.ops[e]

            def body(eng, ops=ops):
                for fn, waits, evt, inc in ops:
                    for k, idx in waits:
                        eng.wait_ge(sem(k), val(k, idx))
                    if fn is None:
                        continue
                    ins = fn(eng)
                    if inc == 16:
                        ins.then_inc(sem(evt[0]), 16)
                    elif evt[1] in rank.get(evt[0], ()):
                        ins.then_inc(sem(evt[0]), 1)
            getters[e](body)


def build_program():
    nc = bass.Bass("TRN2", target_bir_lowering=False)
    S = Sched()
    stack = contextlib.ExitStack()

    def din(name, shape):
        return nc.dram_tensor(name, list(shape), F32, kind="ExternalInput").ap()

    def dout(name, shape):
        return nc.dram_tensor(name, list(shape), F32, kind="ExternalOutput").ap()

    xpT = din("xpT", [NPASS, D, TP])
    keep_d = din("keep", [128, 1])
    xsT = din("xsT", [D, TS])
    sgla = din("sgla", [2, NSEQ_S, 4, 128, 256])
    sconvT = din("sconvT", [2, D, 2 * NSEQ_S])
    w_in_even = din("w_in_even", [2, D, EVEN_IN])
    w_out_even = din("w_out_even", [2, D, D])
    w_in_odd = din("w_in_odd", [2, D, 3 * D])
    w_out_odd = din("w_out_odd", [2, D, D])
    w_ffn_up = din("w_ffn_up", [4, D, DFF])
    w_ffn_down = din("w_ffn_down", [4, DFF, D])
    w_gate_up = din("w_gate_up", [16, 2, 512])
    gains_d = din("gains", [128, 9, 16])
    ggla_d = din("ggla", [128, 2, 8])
    convw_d = din("convw", [128, 2, 3, 16])
    bgate_d = din("bgate", [128, 2, 4])
    wspT_d = din("wspT", [128, 2, 4, 128])
    wsps_d = din("wsps", [64, 2, 4, 64])
    bsp_d = din("bsp", [128, 2, 4, 128])
    bsps_d = din("bsps", [128, 2, 4, 64])
    consts_d = din("consts", [128, NCONST])

    yT = dout("yT", [D, 2 * TP])
    ysT = dout("ysT", [D, TS])
    gp = dout("gp", [2, 4, 128, 256])
    gs = dout("gs", [2, NSEQ_S, 4, 128, 256])
    cpT = dout("cpT", [2, D, 2])
    csT = dout("csT", [2, D, 2 * NSEQ_S])
    cv = dout("cv", [2, TS, 1024])

    def sb(name, shape, dt):
        return stack.enter_context(nc.sbuf_tensor(name, list(shape), dt))

    xT = sb("xT", [128, KC, TT], F32)
    hT = sb("hT", [128, KC * TT], BF16)
    wsl = sb("wsl", [128, NSLOT, KC, WCOLS], BF16)
    S32 = sb("S32", [128, 2, 4, 256], F32)
    Sbf = sb("Sbf", [128, 2, 4, 256], BF16)
    convh = sb("convh", [128, 2, KC, 2], F32)
    rstd = sb("rstd", [128, TT], F32)
    sqb = sb("sqb", [128, 2, TT], BF16)
    cst = sb("cst", [128, NCONST], F32)
    identb = sb("identb", [128, 128], BF16)
    onesb = sb("onesb", [128, 128], BF16)
    gains = sb("gains_s", [128, 9, 16], F32)
    ggla = sb("ggla_s", [128, 2, 8], F32)
    convw = sb("convw_s", [128, 2, 3, 16], F32)
    negb = sb("negb", [128, 2, 4], F32)
    wg = sb("wg", [16, 2, 512], BF16)
    wTm = sb("wTm", [128, 2, 4, 128], BF16)
    Wbd = sb("Wbd", [64, 2, 4, 64], BF16)
    bsp = sb("bsp_s", [128, 2, 4, 128], F32)
    bsps = sb("bsps_s", [128, 2, 4, 64], F32)
    v32 = sb("v32", [64, 1024], F32)
    Sj32 = sb("Sj32", [128, 2, 4, 256], F32)
    eGl = sb("eGl", [128, 4, 8 + NSEQ_S], F32)
    cstage = sb("cstage", [128, KC, NSEQ_S, 2], F32)
    epsb = sb("epsb", [128, 1], F32)
    keep = sb("keep_s", [128, 1], F32)
    xsave = sb("xsave", [128, KC, TS], F32)
    catsave = sb("catsave", [128, KC, TS], BF16)
    WORK_BYTES = 68 * 1024
    work = sb("work", [128, WORK_BYTES // 4], F32)

    ps = stack.enter_context(nc.psum_tensor("ps", [128, 8, 512], F32))

    ident32 = cst[:, C_ID:C_ID + 128]
    m128 = cst[:, C_M128:C_M128 + 128]
    m64 = cst[:, C_M64:C_M64 + 128]
    ms = cst[0:64, C_MS:C_MS + 64]
    onehot = cst[0:64, C_OH:C_OH + 16]
    rmask = cst[:, C_RM:C_RM + TT]

    def carve(off_bytes, n, dt):
        if dt == F32:
            assert off_bytes % 4 == 0
            a = work[:, off_bytes // 4: off_bytes // 4 + n]
            return a
        assert off_bytes % 4 == 0 and n % 2 == 0
        a = work[:, off_bytes // 4: off_bytes // 4 + n // 2]
        return a.bitcast(BF16)

    def hcarve(off_bytes, n, dt):
        if dt == BF16:
            return hT[:, off_bytes // 2: off_bytes // 2 + n]
        a = hT[:, off_bytes // 2: off_bytes // 2 + 2 * n]
        return a.bitcast(F32)

    pool_state = {"big": 0, "small": 0, "pinned": set()}

    def alloc_big():
        while True:
            b = pool_state["big"] % 8
            pool_state["big"] += 1
            if b not in pool_state["pinned"]:
                break
        return ps[:, b, :], ("psb", b)

    def alloc_for(n):
        return alloc_big()

    plan = []
    wstate = {"next": 0, "issued": 0}

    def slot_view(s, kcs, ncols):
        if kcs == KC:
            return wsl[:, s, :, 0:ncols]
        return wsl[:, s].rearrange("p k n -> p (k n)").rearrange("p (k n) -> p k n", n=ncols)

    def w_issue(j):
        ap2d, ncols, kcs = plan[j]
        s = j % NSLOT
        src = ap2d.rearrange("(kc p) n -> p kc n", p=128)
        dst = slot_view(s, kcs, ncols)
        S.dma("pool", lambda e, dst=dst, src=src: e.dma_start(out=dst, in_=src),
              key=("w", s), wr=[("wsl", s)])

    def w_next(ap2d, ncols, kcs=KC):
        if S.dry:
            plan.append((ap2d, ncols, kcs))
            return slot_view(0, kcs, ncols), ("wsl", 0)
        i = wstate["next"]
        wstate["next"] += 1
        while wstate["issued"] < min(len(plan), i + NSLOT):
            w_issue(wstate["issued"])
            wstate["issued"] += 1
        s = i % NSLOT
        return slot_view(s, kcs, ncols), ("wsl", s)

    evac_rr = {"i": 0}

    def rr_eng():
        evac_rr["i"] += 1
        return "act" if evac_rr["i"] % 2 else "dve"

    def copy_op(eng, out, in_, rd, wr):
        if eng == "act":
            S.op("act", lambda e: e.activation(out=out, in_=in_, func=AF.Copy), rd=rd, wr=wr)
        else:
            S.op("dve", lambda e: e.tensor_copy(out=out, in_=in_), rd=rd, wr=wr)

    def gemm_fm(tiles, chunk_list, rhs_fn):
        for spec in chunk_list:
            kq_srcs, ncols, evac = spec[:3]
            kcs = spec[3] if len(spec) > 3 else KC
            nnb = max(1, ncols // 128)
            mcols = min(ncols, 128)
            acc = {}
            shared = None
            for nb in range(nnb):
                for ti, (c0, n) in enumerate(tiles):
                    if n <= 64 and nnb == 4:
                        if shared is None:
                            shared = alloc_big()
                        acc[(nb, ti)] = (shared[0][:, 64 * nb:64 * (nb + 1)], shared[1], True)
                    else:
                        pa, pr = alloc_big()
                        acc[(nb, ti)] = (pa, pr, False)
            nkq = len(kq_srcs)
            for kq, src in enumerate(kq_srcs):
                slot, sres = w_next(src, ncols, kcs)
                for nb in range(nnb):
                    for ti, (c0, n) in enumerate(tiles):
                        pa, pr, sh = acc[(nb, ti)]
                        for kc in range(kcs):
                            rap, rres = rhs_fn(kq * kcs + kc, ti, c0, n)
                            first = (kq == 0 and kc == 0)
                            last = (kq == nkq - 1 and kc == kcs - 1)
                            if sh:
                                S.op("pe", lambda e, pa=pa, slot=slot, kc=kc, nb=nb, rap=rap, n=n, st=(first and nb == 0):
                                     e.matmul(pa[:, 0:n], lhsT=slot[:, kc, nb * 128: nb * 128 + 128], rhs=rap,
                                              start=st, stop=False, skip_group_check=True),
                                     rd=[sres, rres], wr=[pr])
                            else:
                                S.op("pe", lambda e, pa=pa, slot=slot, kc=kc, nb=nb, rap=rap, n=n, mcols=mcols, first=first, last=last:
                                     e.matmul(pa[0:mcols, 0:n], lhsT=slot[:, kc, nb * 128: nb * 128 + mcols], rhs=rap,
                                              start=first, stop=last),
                                     rd=[sres, rres], wr=[pr])
            for nb in range(nnb):
                for ti, (c0, n) in enumerate(tiles):
                    pa, pr, sh = acc[(nb, ti)]
                    evac(nb, ti, pa, pr, c0, n)

    def wide(W2d, r0, nrows, c0):
        return [W2d[r0 + 1024 * q: r0 + 1024 * (q + 1), c0:c0 + 512] for q in range(nrows // 1024)]

    def rmsnorm(tiles, gidx, out_fn):
        for ti, (c0, n) in enumerate(tiles):
            pa, pr = alloc_for(n)
            for kc in range(KC):
                sq = sqb[:, kc % 2, 0:n]
                S.op("act", lambda e, sq=sq, kc=kc, c0=c0, n=n: e.activation(out=sq, in_=xT[:, kc, c0:c0 + n], func=AF.Square),
                     rd=[("xT", kc, ti)], wr=[("sqb", kc % 2)])
                S.op("pe", lambda e, pa=pa, sq=sq, kc=kc, n=n: e.matmul(pa[:, 0:n], lhsT=onesb[:, :], rhs=sq,
                                                                       start=(kc == 0), stop=(kc == KC - 1)),
                     rd=[("sqb", kc % 2), "onesb"], wr=[pr])
            S.op("act", lambda e, pa=pa, c0=c0, n=n: e.activation(out=rstd[:, c0:c0 + n], in_=pa[:, 0:n], func=AF.Sqrt,
                                                                 scale=1.0 / D, bias=epsb[:, 0:1]),
                 rd=[pr, "epsb"], wr=[("rstd", ti)])
            S.op("dve", lambda e, c0=c0, n=n: e.reciprocal(out=rstd[:, c0:c0 + n], in_=rstd[:, c0:c0 + n]),
                 rd=[("rstd", ti)], wr=[("rstd", ti)])
            for kc in range(KC):
                oap, ores = out_fn(kc, ti, c0, n)
                S.op("dve", lambda e, oap=oap, kc=kc, c0=c0, n=n: e.scalar_tensor_tensor(
                    out=oap, in0=xT[:, kc, c0:c0 + n], scalar=gains[:, gidx, kc:kc + 1], in1=rstd[:, c0:c0 + n],
                    op0=ALU.mult, op1=ALU.mult), rd=[("xT", kc, ti), ("rstd", ti), "gains"], wr=[ores])

    hT3 = hT[:, :].rearrange("p (k t) -> p k t", t=TT)

    def h_out(kc, ti, c0, n):
        return hT3[:, kc, c0:c0 + n], ("hT", kc, ti)

    def h_rhs(kc, ti, c0, n):
        return hT3[:, kc, c0:c0 + n], ("hT", kc, ti)

    def add_to_x(nb_glob):
        def evac(nb, ti, pa, pr, c0, n, nb_glob=nb_glob):
            g = nb_glob + nb
            S.op("dve", lambda e: e.tensor_tensor(out=xT[:, g, c0:c0 + n], in0=pa[:, 0:n], in1=xT[:, g, c0:c0 + n], op=ALU.add),
                 rd=[pr, ("xT", g, ti)], wr=[("xT", g, ti)])
        return evac

    def setup():
        def ld(dst, src, res):
            S.dma("sp", lambda e: e.dma_start(out=dst, in_=src), key=("c", res), wr=[res])
        ld(cst[:, :], consts_d[:, :], "cst")
        ld(keep[:, :], keep_d[:, :], "keep")
        ld(gains[:, :, :], gains_d[:, :, :], "gains")
        ld(ggla[:, :, :], ggla_d[:, :, :], "ggla")
        ld(convw[:, :, :, :], convw_d[:, :, :, :], "convw")
        ld(negb[:, :, :], bgate_d[:, :, :], "negb")
        ld(bsp[:, :, :, :], bsp_d[:, :, :, :], "bsp")
        ld(bsps[:, :, :, :], bsps_d[:, :, :, :], "bsps")
        wsp32 = carve(0, 2 * 4 * 128, F32).rearrange("p (i h t) -> p i h t", i=2, h=4)
        wss32 = carve(4096, 2 * 4 * 64, F32)[0:64].rearrange("p (i h t) -> p i h t", i=2, h=4)
        ld(wsp32, wspT_d[:, :, :, :], "wsp32")
        ld(wss32, wsps_d[:, :, :, :], "wss32")
        S.dma("pool", lambda e: e.dma_start(out=identb[:, :], in_=consts_d[:, C_ID:C_ID + 128]), key=("c", "identb"), wr=["identb"])
        S.dma("pool", lambda e: e.dma_start(out=wg[:, :, :], in_=w_gate_up[:, :, :]), key=("c", "wg"), wr=["wg"])
        S.op("dve", lambda e: e.memset(onesb[:, :], 1.0), wr=["onesb"])
        S.op("dve", lambda e: e.memset(epsb[:, :], EPS), wr=["epsb"])
        S.op("dve", lambda e: e.memset(S32[:, :, :, :], 0.0), wr=[("S32", i, h) for i in range(2) for h in range(4)])
        S.op("dve", lambda e: e.memset(Sbf[:, :, :, :], 0.0), wr=[("Sbf", i, h) for i in range(2) for h in range(4)])
        S.op("dve", lambda e: e.memset(convh[:, :, :, :], 0.0), wr=[("convh", i, g) for i in range(2) for g in range(KC)])
        S.op("dve", lambda e: e.tensor_scalar(out=negb[:, :, :], in0=negb[:, :, :], scalar1=-1.0, scalar2=None, op0=ALU.mult),
             rd=["negb"], wr=["negb"])
        for i in range(2):
            for h in range(4):
                S.op("dve", lambda e, i=i, h=h: e.tensor_tensor(out=wTm[:, i, h, :], in0=wsp32[:, i, h, :], in1=m128, op=ALU.mult),
                     rd=["wsp32", "cst"], wr=["wTm"])
                S.op("dve", lambda e, i=i, h=h: e.tensor_tensor(out=Wbd[:, i, h, :], in0=wss32[:, i, h, :], in1=ms, op=ALU.mult),
                     rd=["wss32", "cst"], wr=["Wbd"])

    O_QIN = 0
    O_KIN = O_QIN + 4 * TT * 2
    O_KDT = O_KIN + 4 * TT * 2
    O_Q32 = O_KDT + 4 * TT * 2
    O_GLR = O_Q32 + 4 * TS * 4
    O_RT = O_GLR + TT * 2
    O_R2 = O_RT + 8 * TT * 2
    O_CS = O_R2
    O_EG = O_CS + 4 * TT * 4
    O_EGN = O_EG + 4 * TT * 4
    O_EGD = O_EGN + 4 * TT * 4
    O_E1 = O_EGD + 4 * TT * 4
    O_SP = O_E1 + TT * 4
    O_DD = O_SP + TT * 4
    END_P1 = O_DD + TT * 4
    O_CAT = O_R2
    O_VB = O_CAT + 16 * TT * 2
    O_VT = O_VB + 5 * 1024 * 2
    O_T1 = O_VT + 5 * 1024 * 2
    END_PA = O_T1 + 2 * 512 * 4
    O_KDTOK = O_VT
    O_KDTS = O_KDTOK + 4 * 512 * 2
    O_KDM = O_KDTS + 512 * 2
    END_PB = O_KDM + 16 * 128 * 2
    assert END_PB <= O_T1, (END_PB, O_T1)
    assert max(END_P1, END_PA) <= WORK_BYTES, (END_P1, END_PA, WORK_BYTES)
    H_OT = 0
    H_OSQ = H_OT + 2 * TT * 4
    H_T2 = H_OSQ + 2 * TT * 2
    H_SCM = H_T2 + TT * 4
    H_SCMS = H_SCM + 2 * 128 * 2
    H_KDM2 = H_SCMS + 64 * 2
    H_KDM3 = H_KDM2 + 16 * 128 * 2
    assert H_KDM3 + 16 * 128 * 2 <= KC * TT * 2, (H_KDM3,)

    fence_res = []

    def even_mixer(i, tiles, pi, state_only=False, out_tiles=None, pre_out=None, save_tail=False):
        has_s = len(tiles) > 1
        Wm = w_in_even[i]
        q_in = carve(O_QIN, 4 * TT, BF16).rearrange("p (h t) -> p h t", h=4)
        k_in = carve(O_KIN, 4 * TT, BF16).rearrange("p (h t) -> p h t", h=4)
        kdT = carve(O_KDT, 4 * TT, BF16).rearrange("p (h t) -> p h t", h=4)
        q32 = carve(O_Q32, 4 * TS, F32).rearrange("p (h t) -> p h t", h=4)
        glrT = carve(O_GLR, TT, BF16)
        rT = carve(O_RT, 8 * TT, BF16).rearrange("p (c t) -> p c t", c=8)
        cs = carve(O_CS, 4 * TT, F32).rearrange("p (h t) -> p h t", h=4)
        eG = carve(O_EG, 4 * TT, F32).rearrange("p (h t) -> p h t", h=4)
        eGn = carve(O_EGN, 4 * TT, F32).rearrange("p (h t) -> p h t", h=4)
        egd = carve(O_EGD, 4 * TT, F32).rearrange("p (h t) -> p h t", h=4)
        e1 = carve(O_E1, TT, F32)
        spb = carve(O_SP, TT, F32)
        dd = carve(O_DD, TT, F32)
        catT = carve(O_CAT, 16 * TT, BF16).rearrange("p (c t) -> p c t", c=16)
        vb_tok = carve(O_VB, 5 * 1024, BF16).rearrange("p (j n) -> p j n", j=5)
        v_tok = carve(O_VT, 5 * 1024, BF16).rearrange("p (j n) -> p j n", j=5)
        t1 = carve(O_T1, 2 * 512, F32).rearrange("p (b t) -> p b t", b=2)
        kd_tok = carve(O_KDTOK, 4 * 512, BF16).rearrange("p (j n) -> p j n", j=4)
        kdts = carve(O_KDTS, 512, BF16)
        kdm_bufs = [carve(O_KDM, 16 * 128, BF16).rearrange("p (j d) -> p j d", j=16),
                    carve(O_T1, 16 * 128, BF16).rearrange("p (j d) -> p j d", j=16),
                    hcarve(H_KDM2, 16 * 128, BF16).rearrange("p (j d) -> p j d", j=16),
                    hcarve(H_KDM3, 16 * 128, BF16).rearrange("p (j d) -> p j d", j=16)]
        oT = hcarve(H_OT, 2 * TT, F32).rearrange("p (c t) -> p c t", c=2)
        osq = hcarve(H_OSQ, 2 * TT, BF16).rearrange("p (c t) -> p c t", c=2)
        t2 = hcarve(H_T2, TT, F32)
        scm = hcarve(H_SCM, 2 * 128, BF16).rearrange("p (b t) -> p b t", b=2)
        scms = hcarve(H_SCMS, 64, BF16)

        if fence_res:
            for eng in ("act", "dve"):
                S.fence(eng, list(fence_res))
            del fence_res[:]
        S.barrier()
        def ev_glr(nb, ti, pa, pr, c0, n):
            S.op("act", lambda e: e.activation(out=glrT[0:16, c0:c0 + n], in_=pa[0:16, 0:n], func=AF.Copy),
                 rd=[pr], wr=[("glrT", ti)])
        gemm_fm(tiles, [([Wm[:, 5120:5136]], 16, ev_glr)], h_rhs)
        for ti, (c0, n) in enumerate(tiles):
            L = 64 if ti == 0 else 4
            nch = n // L
            for h in range(4):
                pa, pr = alloc_for(n)
                S.op("pe", lambda e, pa=pa, h=h, c0=c0, n=n: e.matmul(pa[:, 0:n], lhsT=wg[0:16, i, h * 128:(h + 1) * 128],
                                                                     rhs=glrT[0:16, c0:c0 + n], start=True, stop=True),
                     rd=[("glrT", ti), "wg"], wr=[pr])
                S.op("act", lambda e, pa=pa, h=h, n=n: e.activation(out=e1[:, 0:n], in_=pa[:, 0:n], func=AF.Exp, scale=-1.0,
                                                                   bias=negb[:, i, h:h + 1]), rd=[pr, "negb"], wr=["e1"])
                S.op("act", lambda e, n=n: e.activation(out=spb[:, 0:n], in_=e1[:, 0:n], func=AF.Ln, scale=1.0, bias=1.0),
                     rd=["e1"], wr=["spb"])
                S.op("dve", lambda e, h=h, c0=c0, n=n: e.tensor_tensor_scan(out=cs[:, h, c0:c0 + n], data0=rmask[:, c0:c0 + n],
                                                                           data1=spb[:, 0:n], initial=0.0, op0=ALU.mult, op1=ALU.add),
                     rd=["spb", "cst"], wr=[("cs", h, ti)])
                S.op("act", lambda e, h=h, c0=c0, n=n: e.activation(out=eG[:, h, c0:c0 + n], in_=cs[:, h, c0:c0 + n], func=AF.Exp,
                                                                   scale=-1.0 / 16), rd=[("cs", h, ti)], wr=[("eG", h, ti)])
                S.op("act", lambda e, h=h, c0=c0, n=n: e.activation(out=eGn[:, h, c0:c0 + n], in_=cs[:, h, c0:c0 + n], func=AF.Exp,
                                                                   scale=1.0 / 16), rd=[("cs", h, ti)], wr=[("eGn", h, ti)])
                csv = cs[:, h, c0:c0 + n].rearrange("p (c t) -> p c t", t=L)
                S.op("dve", lambda e, csv=csv, n=n, L=L, nch=nch: e.tensor_tensor(
                    out=dd[:, 0:n].rearrange("p (c t) -> p c t", t=L), in0=csv[:, :, L - 1:L].broadcast_to([128, nch, L]),
                    in1=csv, op=ALU.subtract), rd=[("cs", h, ti)], wr=["dd"])
                S.op("act", lambda e, h=h, c0=c0, n=n: e.activation(out=egd[:, h, c0:c0 + n], in_=dd[:, 0:n], func=AF.Exp,
                                                                   scale=-1.0 / 16), rd=["dd"], wr=[("egd", h, ti)])
                eo = 0 if ti == 0 else 8
                egv = eG[:, h, c0:c0 + n].rearrange("p (c t) -> p c t", t=L)
                S.op("dve", lambda e, h=h, egv=egv, eo=eo, nch=nch, L=L: e.tensor_copy(
                    out=eGl[:, h, eo:eo + nch].unsqueeze(2), in_=egv[:, :, L - 1:L]), rd=[("eG", h, ti)], wr=[("eGl", h, ti)])
        def ev_q(base):
            def ev(nb, ti, pa, pr, c0, n):
                h = base + nb
                S.op("dve", lambda e: e.scalar_tensor_tensor(out=q_in[:, h, c0:c0 + n], in0=pa[:, 0:n], scalar=float(128 ** -0.5),
                                                             in1=eG[:, h, c0:c0 + n], op0=ALU.mult, op1=ALU.mult),
                     rd=[pr, ("eG", h, ti)], wr=[("q_in", h, ti)])
                if ti == 1:
                    S.op("dve", lambda e: e.scalar_tensor_tensor(out=q32[:, h, 0:n], in0=pa[:, 0:n], scalar=float(128 ** -0.5),
                                                                 in1=eG[:, h, c0:c0 + n], op0=ALU.mult, op1=ALU.mult),
                         rd=[pr, ("eG", h, ti)], wr=[("q32", h)])
            return ev

        def ev_k(base):
            def ev(nb, ti, pa, pr, c0, n):
                h = base + nb
                if not state_only:
                    S.op("dve", lambda e: e.tensor_tensor(out=k_in[:, h, c0:c0 + n], in0=pa[:, 0:n], in1=eGn[:, h, c0:c0 + n], op=ALU.mult),
                         rd=[pr, ("eGn", h, ti)], wr=[("k_in", h, ti)])
                S.op("dve", lambda e: e.tensor_tensor(out=kdT[:, h, c0:c0 + n], in0=pa[:, 0:n], in1=egd[:, h, c0:c0 + n], op=ALU.mult),
                     rd=[pr, ("egd", h, ti)], wr=[("kdT", h, ti)])
            return ev

        def ev_act(dst, base, func, name):
            def ev(nb, ti, pa, pr, c0, n):
                c = base + nb
                S.op("act", lambda e: e.activation(out=dst[:, c, c0:c0 + n], in_=pa[:, 0:n], func=func),
                     rd=[pr], wr=[(name, c, ti)])
            return ev
        chunks = []
        if not state_only:
            chunks.append((wide(Wm, 0, D, 2048), 512, ev_q(0), 8))
        chunks.append((wide(Wm, 0, D, 2560), 512, ev_k(0), 8))
        gemm_fm(tiles, chunks, h_rhs)
        S.barrier(("act", "dve"))
        chunks = []
        for b in range(0 if state_only else 2):
            chunks.append((wide(Wm, 0, D, 4096 + 512 * b), 512, ev_act(rT, 4 * b, AF.Silu, "rT"), 8))
        for b in range(0 if state_only else 2):
            chunks.append((wide(Wm, 0, D, 512 * b), 512, ev_act(catT, 4 * b, AF.Gelu_apprx_tanh, "cat"), 8))
        gemm_fm(tiles, chunks, h_rhs)
        subt = [(128 * j, 128) for j in range(4)] + ([(TP, TS)] if has_s else [])
        for which in ((1,) if state_only else (0, 1)):
            for b in range(4):
                col0 = (1024 if which == 0 else 3072) + 256 * b
                slot, sres = w_next(Wm[:, col0:col0 + 256], 256)
                for j, (c0, m) in enumerate(subt):
                    pa, pr = alloc_big()
                    for kc in range(KC):
                        S.op("pe", lambda e, pa=pa, slot=slot, kc=kc, c0=c0, m=m: e.matmul(
                            pa[0:m, 0:256], lhsT=hT3[:, kc, c0:c0 + m], rhs=slot[:, kc, 0:256], start=(kc == 0), stop=(kc == KC - 1)),
                            rd=[sres, ("hT", kc, 0 if j < 4 else 1)], wr=[pr])
                    if which == 0:
                        if j < 4:
                            S.op("act", lambda e, pa=pa, j=j, b=b, m=m: e.activation(out=v_tok[0:m, j, 256 * b:256 * (b + 1)], in_=pa[0:m, 0:256],
                                                                                    func=AF.Gelu_apprx_tanh), rd=[pr], wr=[("v_tok", j, b)])
                        else:
                            S.op("act", lambda e, pa=pa, b=b, m=m: e.activation(out=v32[0:m, 256 * b:256 * (b + 1)], in_=pa[0:m, 0:256],
                                                                               func=AF.Gelu_apprx_tanh), rd=[pr], wr=[("v32", b)])
                            S.op("dve", lambda e, b=b, m=m, j=j: e.tensor_copy(out=v_tok[0:m, j, 256 * b:256 * (b + 1)], in_=v32[0:m, 256 * b:256 * (b + 1)]),
                                 rd=[("v32", b)], wr=[("v_tok", j, b)])
                    else:
                        copy_op(rr_eng(), vb_tok[0:m, j, 256 * b:256 * (b + 1)], pa[0:m, 0:256], [pr], [("vb_tok", j, b)])
        if has_s:
            S.dma("sp", lambda e: e.dma_start(out=cv[i, :, :], in_=v32[:, :]), key=("cv",), rd=[("v32", b) for b in range(4)], final=True)
        for c in range(0 if state_only else 8):
            h = c // 2
            pa, pr = alloc_big()
            for j in range(4):
                S.op("pe", lambda e, pa=pa, j=j, c=c, h=h: e.matmul(pa[:, 128 * j:128 * (j + 1)], lhsT=v_tok[:, j, 128 * c:128 * (c + 1)],
                                                                   rhs=wTm[:, i, h, :], start=True, stop=True),
                     rd=[("v_tok", j, c // 2), "wTm"], wr=[pr])
            tb = t1[:, c % 2, :]
            S.op("dve", lambda e, pa=pa, tb=tb, h=h: e.tensor_tensor(
                out=tb.rearrange("p (j t) -> p j t", j=4), in0=pa[:, :].rearrange("p (j t) -> p j t", j=4),
                in1=bsp[:, i, h, :].unsqueeze(1).broadcast_to([128, 4, 128]), op=ALU.add), rd=[pr, "bsp"], wr=[("t1", c % 2)])
            S.op("dve", lambda e, tb=tb, c=c: e.tensor_tensor(out=catT[:, c, 0:TP], in0=tb, in1=catT[:, c, 0:TP], op=ALU.mult),
                 rd=[("t1", c % 2), ("cat", c, 0)], wr=[("cat", c, 0)])
            if has_s:
                pa2, pr2 = alloc_big()
                S.op("pe", lambda e, pa2=pa2, c=c, h=h: e.matmul(pa2[:, 0:TS], lhsT=v_tok[0:TS, 4, 128 * c:128 * (c + 1)],
                                                                rhs=Wbd[:, i, h, :], start=True, stop=True),
                     rd=[("v_tok", 4, c // 2), "Wbd"], wr=[pr2])
                S.op("dve", lambda e, pa2=pa2, tb=tb, h=h: e.tensor_tensor(out=tb[:, 0:TS], in0=pa2[:, 0:TS], in1=bsps[:, i, h, :], op=ALU.add),
                     rd=[pr2, "bsps"], wr=[("t1", c % 2)])
                S.op("dve", lambda e, tb=tb, c=c: e.tensor_tensor(out=catT[:, c, TP:TT], in0=tb[:, 0:TS], in1=catT[:, c, TP:TT], op=ALU.mult),
                     rd=[("t1", c % 2), ("cat", c, 1)], wr=[("cat", c, 1)])
        S.barrier()
        for j in range(4):
            pa, pr = alloc_big()
            pab = pa.bitcast(BF16)
            for h in range(4):
                S.op("pe", lambda e, pab=pab, h=h, j=j: e.transpose(pab[:, 128 * h:128 * (h + 1)], kdT[:, h, 128 * j:128 * (j + 1)], identb[:, :]),
                     rd=[("kdT", h, 0), "identb"], wr=[pr])
            copy_op(rr_eng(), kd_tok[:, j, :], pab[:, 0:512], [pr], [("kd_tok", j)])

        def norm_head(h, c0, n, ti):
            pn, prn = alloc_big()
            for dc in range(2):
                S.op("pe", lambda e, pn=pn, dc=dc, c0=c0, n=n: e.matmul(pn[:, 0:n], lhsT=onesb[:, :], rhs=osq[:, dc, c0:c0 + n],
                                                                       start=(dc == 0), stop=(dc == 1)),
                     rd=[("osq", dc, ti), "onesb"], wr=[prn])
            S.op("act", lambda e, pn=pn, c0=c0, n=n: e.activation(out=t2[:, c0:c0 + n], in_=pn[:, 0:n], func=AF.Sqrt, scale=1.0 / 256, bias=epsb[:, 0:1]),
                 rd=[prn, "epsb"], wr=[("t2", ti)])
            S.op("dve", lambda e, c0=c0, n=n: e.reciprocal(out=t2[:, c0:c0 + n], in_=t2[:, c0:c0 + n]), rd=[("t2", ti)], wr=[("t2", ti)])
            for dc in range(2):
                c = 2 * h + dc
                S.op("dve", lambda e, dc=dc, c=c, c0=c0, n=n: e.scalar_tensor_tensor(
                    out=oT[:, dc, c0:c0 + n], in0=oT[:, dc, c0:c0 + n], scalar=ggla[:, i, c:c + 1], in1=t2[:, c0:c0 + n],
                    op0=ALU.mult, op1=ALU.mult), rd=[("oT", dc, ti), ("t2", ti), "ggla"], wr=[("oT", dc, ti)])
                S.op("dve", lambda e, dc=dc, c=c, c0=c0, n=n: e.tensor_tensor(out=catT[:, 8 + c, c0:c0 + n], in0=oT[:, dc, c0:c0 + n],
                                                                             in1=rT[:, c, c0:c0 + n], op=ALU.mult),
                     rd=[("oT", dc, ti), ("rT", c, ti)], wr=[("cat", 8 + c, ti)])

        if state_only:
            for h in range(4):
                for j in range(4):
                    for cc in range(2):
                        pS, prS = alloc_big()
                        S.op("pe", lambda e, pS=pS, cc=cc, j=j, h=h: e.matmul(
                            pS[:, 0:256], lhsT=kd_tok[64 * cc:64 * (cc + 1), j, 128 * h:128 * (h + 1)],
                            rhs=vb_tok[64 * cc:64 * (cc + 1), j, 256 * h:256 * (h + 1)], start=True, stop=True),
                            rd=[("kd_tok", j), ("vb_tok", j, h)], wr=[prS])
                        ch = 2 * j + cc
                        S.op("dve", lambda e, pS=pS, h=h, ch=ch: e.scalar_tensor_tensor(
                            out=S32[:, i, h, :], in0=S32[:, i, h, :], scalar=eGl[:, h, ch:ch + 1], in1=pS[:, 0:256],
                            op0=ALU.mult, op1=ALU.add), rd=[prS, ("S32", i, h), ("eGl", h, 0)], wr=[("S32", i, h)])
                S.op("act", lambda e, h=h: e.activation(out=Sbf[:, i, h, :], in_=S32[:, i, h, :], func=AF.Copy),
                     rd=[("S32", i, h)], wr=[("Sbf", i, h)])
        for h in range(0 if state_only else 4):
            for j in range(4):
                p1, pr1 = alloc_big()
                S.op("pe", lambda e, p1=p1, h=h, j=j: e.matmul(p1[:, 0:128], lhsT=k_in[:, h, 128 * j:128 * (j + 1)],
                                                              rhs=q_in[:, h, 128 * j:128 * (j + 1)], start=True, stop=True),
                     rd=[("k_in", h, 0), ("q_in", h, 0)], wr=[pr1])
                sc = scm[:, j % 2, :]
                S.op("dve", lambda e, p1=p1, sc=sc: e.tensor_tensor(out=sc, in0=p1[:, 0:128], in1=m64, op=ALU.mult),
                     rd=[pr1, "cst"], wr=[("scm", j % 2)])
                for cc in range(2):
                    pS, prS = alloc_big()
                    S.op("pe", lambda e, pS=pS, cc=cc, j=j, h=h: e.matmul(
                        pS[:, 0:256], lhsT=kd_tok[64 * cc:64 * (cc + 1), j, 128 * h:128 * (h + 1)],
                        rhs=vb_tok[64 * cc:64 * (cc + 1), j, 256 * h:256 * (h + 1)], start=True, stop=True),
                        rd=[("kd_tok", j), ("vb_tok", j, h)], wr=[prS])
                    if cc == 0:
                        po = []
                        for dc in range(2):
                            c = 2 * h + dc
                            pq, prq = alloc_big()
                            po.append((pq, prq))
                            S.op("pe", lambda e, pq=pq, c=c, j=j, sc=sc: e.matmul(pq[:, 0:128], lhsT=vb_tok[:, j, 128 * c:128 * (c + 1)],
                                                                              rhs=sc, start=True, stop=False),
                                 rd=[("vb_tok", j, h), ("scm", j % 2)], wr=[prq])
                            S.op("pe", lambda e, pq=pq, dc=dc, h=h, j=j: e.matmul(pq[:, 0:64], lhsT=Sbf[:, i, h, 128 * dc:128 * (dc + 1)],
                                                                              rhs=q_in[:, h, 128 * j:128 * j + 64], start=False, stop=False),
                                 rd=[("Sbf", i, h), ("q_in", h, 0)], wr=[prq])
                    ch = 2 * j + cc
                    S.op("dve", lambda e, pS=pS, h=h, ch=ch: e.scalar_tensor_tensor(
                        out=S32[:, i, h, :], in0=S32[:, i, h, :], scalar=eGl[:, h, ch:ch + 1], in1=pS[:, 0:256],
                        op0=ALU.mult, op1=ALU.add), rd=[prS, ("S32", i, h), ("eGl", h, 0)], wr=[("S32", i, h)])
                    S.op("act", lambda e, h=h: e.activation(out=Sbf[:, i, h, :], in_=S32[:, i, h, :], func=AF.Copy),
                         rd=[("S32", i, h)], wr=[("Sbf", i, h)])
                    if cc == 0:
                        for dc in range(2):
                            pq, prq = po[dc]
                            S.op("pe", lambda e, pq=pq, dc=dc, h=h, j=j: e.matmul(pq[:, 64:128], lhsT=Sbf[:, i, h, 128 * dc:128 * (dc + 1)],
                                                                              rhs=q_in[:, h, 128 * j + 64:128 * (j + 1)], start=False, stop=True),
                                 rd=[("Sbf", i, h), ("q_in", h, 0)], wr=[prq])
                for dc in range(2):
                    pq, prq = po[dc]
                    S.op("act", lambda e, pq=pq, dc=dc, j=j: e.activation(out=oT[:, dc, 128 * j:128 * (j + 1)], in_=pq[:, 0:128], func=AF.Copy),
                         rd=[prq], wr=[("oT", dc, 0)])
                    S.op("act", lambda e, pq=pq, dc=dc, j=j: e.activation(out=osq[:, dc, 128 * j:128 * (j + 1)], in_=pq[:, 0:128], func=AF.Square),
                         rd=[prq], wr=[("osq", dc, 0)])
            norm_head(h, 0, TP, 0)
        if pi == NPASS - 1:
            S.dma("sp", lambda e: e.dma_start(out=gp[i].rearrange("h d e -> d h e"), in_=S32[:, i, :, :]), key=("gp", i),
                  rd=[("S32", i, h) for h in range(4)], final=True)
        if has_s:
            pa, pr = alloc_big()
            pab = pa.bitcast(BF16)
            for h in range(4):
                S.op("pe", lambda e, pab=pab, h=h: e.transpose(pab[0:TS, 128 * h:128 * (h + 1)], kdT[:, h, TP:TT], identb[:, :]),
                     rd=[("kdT", h, 1), "identb"], wr=[pr])
            copy_op("dve", kdts[0:TS, :], pab[0:TS, 0:512], [pr], ["kdts"])
            acc = {}
            accb, accr = alloc_big()
            pool_state["pinned"].add(accr[1])
            for h in range(4):
                p1, pr1 = alloc_big()
                S.op("pe", lambda e, p1=p1, h=h: e.matmul(p1[0:TS, 0:TS], lhsT=k_in[:, h, TP:TT], rhs=q_in[:, h, TP:TT], start=True, stop=True),
                     rd=[("k_in", h, 1), ("q_in", h, 1)], wr=[pr1])
                S.op("dve", lambda e, p1=p1: e.tensor_tensor(out=scms[0:TS, :], in0=p1[0:TS, 0:TS], in1=ms, op=ALU.mult),
                     rd=[pr1, "cst"], wr=["scms"])
                for dc in range(2):
                    c = 2 * h + dc
                    pq, prq = accb[:, 64 * c:64 * (c + 1)], accr
                    acc[(h, dc)] = (pq, prq)
                    S.op("pe", lambda e, pq=pq, c=c: e.matmul(pq[:, 0:TS], lhsT=vb_tok[0:TS, 4, 128 * c:128 * (c + 1)], rhs=scms[0:TS, :],
                                                             start=(c == 0), stop=False, skip_group_check=True),
                         rd=[("vb_tok", 4, h), "scms"], wr=[prq])
            def sj_load(jj):
                bsel = jj % 2
                S.dma("sp", lambda e, jj=jj, bsel=bsel: e.dma_start(out=Sj32[:, bsel, :, :], in_=sgla[i, jj].rearrange("h d e -> d h e")),
                      key=("sj", bsel), wr=[("Sj32", bsel)])
            sj_load(0)
            for jj in range(NSEQ_S):
                bsel = jj % 2
                if jj + 1 < NSEQ_S:
                    sj_load(jj + 1)
                for h in range(4):
                    for dc in range(2):
                        pq, prq = acc[(h, dc)]
                        S.op("pe", lambda e, pq=pq, h=h, dc=dc, jj=jj, bsel=bsel: e.matmul(
                            pq[:, 4 * jj:4 * jj + 4], lhsT=Sj32[:, bsel, h, 128 * dc:128 * (dc + 1)], rhs=q32[:, h, 4 * jj:4 * jj + 4],
                            start=False, stop=False, skip_group_check=True), rd=[("Sj32", bsel), ("q32", h)], wr=[prq])
                for h in range(4):
                    if jj == 0:
                        S.op("dve", lambda e, h=h: e.tensor_tensor(
                            out=kdm_bufs[h][0:TS], in0=kdts[0:TS, 128 * h:128 * (h + 1)].unsqueeze(1).broadcast_to([TS, NSEQ_S, 128]),
                            in1=onehot.unsqueeze(2).broadcast_to([TS, NSEQ_S, 128]), op=ALU.mult), rd=["kdts", "cst"], wr=[("kdm", h)])
                    pS, prS = alloc_big()
                    S.op("pe", lambda e, pS=pS, h=h, jj=jj: e.matmul(pS[:, 0:256], lhsT=kdm_bufs[h][0:TS, jj, :], rhs=vb_tok[0:TS, 4, 256 * h:256 * (h + 1)],
                                                                    start=True, stop=True), rd=[("kdm", h), ("vb_tok", 4, h)], wr=[prS])
                    S.op("dve", lambda e, pS=pS, h=h, jj=jj, bsel=bsel: e.scalar_tensor_tensor(
                        out=Sj32[:, bsel, h, :], in0=Sj32[:, bsel, h, :], scalar=eGl[:, h, 8 + jj:8 + jj + 1], in1=pS[:, 0:256],
                        op0=ALU.mult, op1=ALU.add), rd=[prS, ("Sj32", bsel), ("eGl", h, 1)], wr=[("Sj32", bsel)])
                S.dma("sp", lambda e, jj=jj, bsel=bsel: e.dma_start(out=gs[i, jj].rearrange("h d e -> d h e"), in_=Sj32[:, bsel, :, :]),
                      key=("sjo", bsel), rd=[("Sj32", bsel)], final=True)
            for h in range(4):
                for dc in range(2):
                    pq, prq = acc[(h, dc)]
                    S.op("act", lambda e, pq=pq, dc=dc: e.activation(out=oT[:, dc, TP:TT], in_=pq[:, 0:TS], func=AF.Copy),
                         rd=[prq], wr=[("oT", dc, 1)])
                    S.op("act", lambda e, pq=pq, dc=dc: e.activation(out=osq[:, dc, TP:TT], in_=pq[:, 0:TS], func=AF.Square),
                         rd=[prq], wr=[("osq", dc, 1)])
                norm_head(h, TP, TS, 1)
            pool_state["pinned"].discard(accr[1])
        def cat_rhs(kc, ti, c0, n):
            return catT[:, kc, c0:c0 + n], ("cat", kc, ti)
        Wo = w_out_even[i]
        if save_tail:
            S.op("dve", lambda e: e.tensor_copy(out=xsave[:, :, :], in_=xT[:, :, TP - TS:TP]), rd=[("xT", kc, 0) for kc in range(KC)], wr=["xsave"])
            S.op("act", lambda e: e.activation(out=catsave[:, :, :], in_=catT[:, :, TP - TS:TP], func=AF.Copy),
                 rd=[("cat", c, 0) for c in range(16)], wr=["catsave"])
        if pre_out is not None:
            S.op("dve", lambda e: e.tensor_scalar(out=catT[:, :, TP:TT], in0=catsave[:, :, :], scalar1=keep[:, 0:1], scalar2=None, op0=ALU.mult),
                 rd=["catsave", "keep"], wr=[("cat", c, 1) for c in range(16)])
            S.op("dve", lambda e: e.tensor_scalar(out=xT[:, :, TP:TT], in0=xsave[:, :, :], scalar1=keep[:, 0:1], scalar2=None, op0=ALU.mult),
                 rd=["xsave", "keep"], wr=[("xT", kc, 1) for kc in range(KC)])
        ot = tiles if out_tiles is None else out_tiles
        if not state_only and ot:
            gemm_fm(ot, [(wide(Wo, 0, D, 512 * b), 512, add_to_x(4 * b), 8) for b in range(4)], cat_rhs)
        S.barrier()

    O_YIN = 0
    O_ZP = O_YIN + 16 * TT * 2
    O_ZPS = O_ZP + 2 * 516 * 4
    O_CG = O_ZPS + 2 * NSEQ_S * 6 * 4
    O_CV = O_CG + 2 * TT * 4
    O_ZTL = O_CV + 2 * TT * 4
    END_ODD = O_ZTL + 2 * TS * 4
    assert END_ODD <= WORK_BYTES

    def conv_mixer(i, tiles, pi, hist_only=False, tail=False, out_tiles=None):
        has_s = len(tiles) > 1 and not tail
        ztl = carve(O_ZTL, 2 * TS, F32).rearrange("p (b t) -> p b t", b=2)
        stash = {}
        Wm = w_in_odd[i]
        yin = carve(O_YIN, 16 * TT, BF16).rearrange("p (c t) -> p c t", c=16)
        zp = carve(O_ZP, 2 * 516, F32).rearrange("p (b t) -> p b t", b=2)
        zps = carve(O_ZPS, 2 * NSEQ_S * 6, F32).rearrange("p (b s t) -> p b s t", b=2, s=NSEQ_S)
        cgb = carve(O_CG, 2 * TT, F32).rearrange("p (b t) -> p b t", b=2)
        cvb = carve(O_CV, 2 * TT, F32).rearrange("p (b t) -> p b t", b=2)
        S.barrier()
        chunks = []
        for b in range(8):
            def ev_cg(nb, ti, pa, pr, c0, n, b=b):
                S.op("act", lambda e: e.activation(out=cgb[:, nb, c0:c0 + n], in_=pa[:, 0:n], func=AF.Copy), rd=[pr], wr=[("cgb", nb, ti)])

            def ev_hx(nb, ti, pa, pr, c0, n, b=b):
                g = 2 * b + nb
                if ti == 0 and hist_only:
                    S.op("dve", lambda e: e.tensor_tensor(out=zp[:, nb, 2 + c0:2 + c0 + n], in0=pa[:, 0:n], in1=cgb[:, nb, c0:c0 + n], op=ALU.mult),
                         rd=[pr, ("cgb", nb, 0)], wr=[("zp", nb)])
                    S.op("dve", lambda e: e.tensor_copy(out=convh[:, i, g, :], in_=zp[:, nb, TP:TP + 2]), rd=[("zp", nb)], wr=[("convh", i, g)])
                    return
                hist, hist_res = convh[:, i, g, :], ("convh", i, g)
                if tail and ti == 0:
                    stash[nb] = (pa, pr)
                    return
                if tail and ti == 1:
                    S.op("dve", lambda e, pat=pa: e.tensor_tensor(out=ztl[:, nb, 0:TS], in0=pat[:, 0:TS], in1=cgb[:, nb, TP:TT], op=ALU.mult),
                         rd=[pr, ("cgb", nb, 1)], wr=[("ztl", nb)])
                    pa, pr = stash[nb]
                    ti = 0
                    hist, hist_res = ztl[:, nb, TS - 2:TS], ("ztl", nb)
                if ti == 0:
                    S.op("dve", lambda e: e.tensor_copy(out=zp[:, nb, 0:2], in_=hist), rd=[hist_res], wr=[("zp", nb)])
                    S.op("dve", lambda e: e.tensor_tensor(out=zp[:, nb, 2:2 + TP], in0=pa[:, 0:TP], in1=cgb[:, nb, 0:TP], op=ALU.mult),
                         rd=[pr, ("cgb", nb, 0)], wr=[("zp", nb)])
                    S.op("dve", lambda e: e.tensor_copy(out=convh[:, i, g, :], in_=zp[:, nb, TP:TP + 2]), rd=[("zp", nb)], wr=[("convh", i, g)])
                    if hist_only:
                        return
                    S.op("dve", lambda e: e.tensor_scalar(out=cvb[:, nb, 0:TP], in0=zp[:, nb, 0:TP], scalar1=convw[:, i, 0, g:g + 1], scalar2=None,
                                                          op0=ALU.mult), rd=[("zp", nb), "convw"], wr=[("cvb", nb, 0)])
                    for t in (1, 2):
                        S.op("dve", lambda e, t=t: e.scalar_tensor_tensor(out=cvb[:, nb, 0:TP], in0=zp[:, nb, t:t + TP], scalar=convw[:, i, t, g:g + 1],
                                                                         in1=cvb[:, nb, 0:TP], op0=ALU.mult, op1=ALU.add),
                             rd=[("zp", nb), ("cvb", nb, 0)], wr=[("cvb", nb, 0)])
                else:
                    S.op("dve", lambda e: e.tensor_copy(out=zps[:, nb, :, 0:2], in_=cstage[:, g, :, :]), rd=[("cstage", g)], wr=[("zps", nb)])
                    S.op("dve", lambda e: e.tensor_tensor(out=zps[:, nb, :, 2:6], in0=pa[:, 0:TS].rearrange("p (s t) -> p s t", t=4),
                                                          in1=cgb[:, nb, TP:TT].rearrange("p (s t) -> p s t", t=4), op=ALU.mult),
                         rd=[pr, ("cgb", nb, 1)], wr=[("zps", nb)])
                    S.op("dve", lambda e: e.tensor_copy(out=cstage[:, g, :, :], in_=zps[:, nb, :, 4:6]), rd=[("zps", nb)], wr=[("cstage", g)])
                    cvs = cvb[:, nb, TP:TT].rearrange("p (s t) -> p s t", t=4)
                    S.op("dve", lambda e: e.tensor_scalar(out=cvs, in0=zps[:, nb, :, 0:4], scalar1=convw[:, i, 0, g:g + 1], scalar2=None,
                                                          op0=ALU.mult), rd=[("zps", nb), "convw"], wr=[("cvb", nb, 1)])
                    for t in (1, 2):
                        S.op("dve", lambda e, t=t: e.scalar_tensor_tensor(out=cvs, in0=zps[:, nb, :, t:t + 4], scalar=convw[:, i, t, g:g + 1],
                                                                         in1=cvs, op0=ALU.mult, op1=ALU.add),
                             rd=[("zps", nb), ("cvb", nb, 1)], wr=[("cvb", nb, 1)])

            def ev_bg(nb, ti, pa, pr, c0, n, b=b):
                g = 2 * b + nb
                if tail and ti == 1:
                    return
                S.op("dve", lambda e: e.tensor_tensor(out=yin[:, g, c0:c0 + n], in0=pa[:, 0:n], in1=cvb[:, nb, c0:c0 + n], op=ALU.mult),
                     rd=[pr, ("cvb", nb, ti)], wr=[("yin", g, ti)])
            chunks.append(([Wm[:, 2048 + 256 * b:2048 + 256 * (b + 1)]], 256, ev_cg))
            chunks.append(([Wm[:, 4096 + 256 * b:4096 + 256 * (b + 1)]], 256, ev_hx))
            if not hist_only:
                chunks.append(([Wm[:, 256 * b:256 * (b + 1)]], 256, ev_bg))
        if has_s:
            S.dma("sp", lambda e: e.dma_start(out=cstage[:, :, :, :].rearrange("p k s r -> p k (s r)"),
                                              in_=sconvT[i].rearrange("(k p) c -> p k c", p=128)),
                  key=("cst_in",), wr=[("cstage", g) for g in range(KC)])
        gemm_fm(tiles, chunks, h_rhs)
        if has_s:
            S.dma("sp", lambda e: e.dma_start(out=csT[i].rearrange("(k p) c -> p k c", p=128),
                                              in_=cstage[:, :, :, :].rearrange("p k s r -> p k (s r)")),
                  key=("cst_out",), rd=[("cstage", g) for g in range(KC)], final=True)
        if pi == NPASS - 1:
            S.dma("sp", lambda e: e.dma_start(out=cpT[i].rearrange("(k p) c -> p k c", p=128), in_=convh[:, i, :, :]),
                  key=("cp", i), rd=[("convh", i, g) for g in range(KC)], final=True)

        def yin_rhs(kc, ti, c0, n):
            return yin[:, kc, c0:c0 + n], ("yin", kc, ti)
        Wo = w_out_odd[i]
        if not hist_only:
            gemm_fm(out_tiles or tiles, [(wide(Wo, 0, D, 512 * b), 512, add_to_x(4 * b), 8) for b in range(4)], yin_rhs)
        S.barrier()

    O_HID = 0
    O_RL = O_HID + 32 * TT * 2
    assert O_RL + 4 * TT * 4 <= WORK_BYTES

    def ffn(l, tiles):
        hid = carve(O_HID, 32 * TT, BF16).rearrange("p (c t) -> p c t", c=32)
        rl = carve(O_RL, 4 * TT, F32).rearrange("p (b t) -> p b t", b=4)
        Wu = w_ffn_up[l]
        Wd = w_ffn_down[l]
        for half in range(2):
            chunks = []
            for b in range(8):
                def ev_up(nb, ti, pa, pr, c0, n, b=b):
                    c = 4 * b + nb
                    S.op("act", lambda e: e.activation(out=rl[:, nb, c0:c0 + n], in_=pa[:, 0:n], func=AF.Relu), rd=[pr], wr=[("rl", nb, ti)])
                    S.op("dve", lambda e: e.tensor_tensor(out=hid[:, c, c0:c0 + n], in0=rl[:, nb, c0:c0 + n], in1=rl[:, nb, c0:c0 + n], op=ALU.mult),
                         rd=[("rl", nb, ti)], wr=[("hid", c, ti)])
                chunks.append((wide(Wu, 0, D, half * 4096 + 512 * b), 512, ev_up, 8))
            gemm_fm(tiles, chunks, h_rhs)

            def hid_rhs(kc, ti, c0, n):
                return hid[:, kc, c0:c0 + n], ("hid", kc, ti)
            chunks = []
            for b in range(4):
                chunks.append((wide(Wd, half * 4096, 4096, 512 * b), 512, add_to_x(4 * b), 8))
            gemm_fm(tiles, chunks, hid_rhs)

    SLOT_S = 3
    SLOT_T = 2
    XT = (TP, TS)

    def emit_pass(pi):
        mode = ("pre0", "pre1", "full", "full")[pi]
        tp = [(0, TP)]
        tiles = tp + ([XT] if pi == SLOT_S else [])
        both = tp + [XT]
        S.new_epoch()
        S.dma("sp", lambda e: e.dma_start(out=xT[:, :, 0:TP], in_=xpT[pi].rearrange("(k p) t -> p k t", p=128)),
              key=("xin",), wr=[("xT", kc, 0) for kc in range(KC)])
        if pi == SLOT_S:
            S.dma("sp", lambda e: e.dma_start(out=xT[:, :, TP:TT], in_=xsT[:, :].rearrange("(k p) t -> p k t", p=128)),
                  key=("xins",), wr=[("xT", kc, 1) for kc in range(KC)])
        if pi == SLOT_T:
            for i in range(2):
                for h in range(4):
                    S.op("dve", lambda e, i=i, h=h: e.tensor_scalar(out=S32[:, i, h, :], in0=S32[:, i, h, :], scalar1=keep[:, 0:1],
                                                                    scalar2=None, op0=ALU.mult), rd=[("S32", i, h), "keep"], wr=[("S32", i, h)])
                    S.op("act", lambda e, i=i, h=h: e.activation(out=Sbf[:, i, h, :], in_=S32[:, i, h, :], func=AF.Copy),
                         rd=[("S32", i, h)], wr=[("Sbf", i, h)])
                S.op("dve", lambda e, i=i: e.tensor_scalar(out=convh[:, i, :, :], in0=convh[:, i, :, :], scalar1=keep[:, 0:1],
                                                           scalar2=None, op0=ALU.mult),
                     rd=[("convh", i, g) for g in range(KC)] + ["keep"], wr=[("convh", i, g) for g in range(KC)])
        for l in range(DEPTH):
            i = l // 2
            if mode == "pre0" and l == 3:
                break
            tailslot = (pi == SLOT_T)
            lt = both if (tailslot and l == 3) else tiles
            ft = both if (tailslot and l == 2) else tiles
            rmsnorm(lt, l, h_out)
            if l % 2 == 0:
                so = (mode == "pre0" and l == 2)
                p1 = (mode == "pre1" and l == 2)
                even_mixer(i, tiles, pi, state_only=so,
                           out_tiles=([] if p1 else (both if (tailslot and l == 2) else None)),
                           pre_out=(True if (tailslot and l == 2) else None), save_tail=p1)
                if so or p1:
                    break
            else:
                conv_mixer(i, lt, pi, tail=(tailslot and l == 3), out_tiles=tiles)
            rmsnorm(ft, 4 + l, h_out)
            ffn(l, ft)
            if l == DEPTH - 1:
                S.barrier()
        if mode != "full":
            S.barrier()
            return
        ystage = carve(0, KC * TT, F32).rearrange("p (k t) -> p k t", k=KC)

        def y_out(kc, ti, c0, n):
            return ystage[:, kc, c0:c0 + n], ("ystage", kc, ti)
        rmsnorm(tiles, 8, y_out)
        yc = (pi - 2) * TP
        S.dma("sp", lambda e: e.dma_start(out=yT[:, yc:yc + TP].rearrange("(k p) t -> p k t", p=128), in_=ystage[:, :, 0:TP]),
              key=("yout",), rd=[("ystage", kc, 0) for kc in range(KC)], final=True)
        if pi == SLOT_S:
            S.dma("sp", lambda e: e.dma_start(out=ysT[:, :].rearrange("(k p) t -> p k t", p=128), in_=ystage[:, :, TP:TT]),
                  key=("youts",), rd=[("ystage", kc, 1) for kc in range(KC)], final=True)
        fence_res.extend(("ystage", kc, ti) for kc in range(KC) for ti in range(len(tiles)))
        S.barrier()

    def emit_all():
        setup()
        for pi in range(NPASS):
            emit_pass(pi)

    S.dry = True
    emit_all()
    S.dry = False
    S.epoch = 0
    del fence_res[:]
    pool_state["big"] = 0
    pool_state["pinned"] = set()
    evac_rr["i"] = 0
    emit_all()
    S.emit(nc, stack)
    stack.close()
    return nc


_NC = None


def _get_nc():
    global _NC
    if _NC is None:
        _NC = build_program()
    return _NC


def _consts():
    c = np.zeros((128, NCONST), np.float32)
    c[:, C_ID:C_ID + 128] = np.eye(128, dtype=np.float32)
    s = np.arange(128)[:, None]
    t = np.arange(128)[None, :]
    c[:, C_M128:C_M128 + 128] = (s <= t)
    c[:, C_M64:C_M64 + 128] = (s <= t) & (s // 64 == t // 64)
    s4 = np.arange(64)[:, None]
    t4 = np.arange(64)[None, :]
    c[0:64, C_MS:C_MS + 64] = (s4 <= t4) & (s4 // 4 == t4 // 4)
    c[0:64, C_OH:C_OH + 16] = (s4 // 4 == np.arange(16)[None, :])
    rm = np.ones(TT, np.float32)
    rm[0:TP:64] = 0.0
    rm[TP:TT:4] = 0.0
    c[:, C_RM:C_RM + TT] = rm[None, :]
    return c


def kernel(x_prompt, x_sample, state_gla, state_conv, norm_mix, norm_ffn, norm_final, w_in_even,
           w_gate_up, b_gate, w_spatial, b_spatial, g_gla_out, w_out_even, w_in_odd, conv_w,
           w_out_odd, w_ffn_up, w_ffn_down):
    f = lambda a: np.ascontiguousarray(np.asarray(a, dtype=np.float32))
    x_prompt, x_sample, state_gla, state_conv = f(x_prompt), f(x_sample), f(state_gla), f(state_conv)
    nc = _get_nc()

    def fm(a, n):
        return np.ascontiguousarray(a.reshape(n, 16, 128).transpose(2, 0, 1))
    gains = fm(np.concatenate([f(norm_mix), f(norm_ffn), f(norm_final)[None]], 0), 9)
    ggla = np.ascontiguousarray(f(g_gla_out).reshape(2, 8, 128).transpose(2, 0, 1))
    convw = np.ascontiguousarray(f(conv_w).reshape(2, 3, 16, 128).transpose(3, 0, 1, 2))
    bgate = np.ascontiguousarray(f(b_gate).reshape(2, 4, 128).transpose(2, 0, 1))
    wsp = f(w_spatial)
    wspT = np.ascontiguousarray(wsp.transpose(3, 0, 1, 2))
    w44 = wsp[:, :, 0:4, 0:4].transpose(3, 0, 1, 2)
    wsps = np.ascontiguousarray(np.tile(w44, (16, 1, 1, 16)))
    bs = f(b_spatial)
    bsp = np.ascontiguousarray(np.broadcast_to(bs[None], (128, 2, 4, 128)))
    bsps = np.ascontiguousarray(np.broadcast_to(np.tile(bs[:, :, 0:4], (1, 1, 16))[None], (128, 2, 4, 64)))
    wgu = np.ascontiguousarray(f(w_gate_up).transpose(1, 0, 2))
    consts = _consts()
    shared = dict(w_in_even=f(w_in_even), w_out_even=f(w_out_even), w_in_odd=f(w_in_odd), w_out_odd=f(w_out_odd),
                  w_ffn_up=f(w_ffn_up), w_ffn_down=f(w_ffn_down), w_gate_up=wgu, gains=gains, ggla=ggla, convw=convw,
                  bgate=bgate, wspT=wspT, wsps=wsps, bsp=bsp, bsps=bsps, consts=consts)
    in_maps = []
    for c in range(NCORES):
        sl = slice(NSEQ_S * c, NSEQ_S * (c + 1))
        m = dict(shared)
        if c in B_CORES:
            p = B_CORES.index(c)
            xt = [x_prompt[p, TP * t:TP * (t + 1)].T for t in range(4)]
            m["keep"] = np.full((128, 1), 1.0, np.float32)
        else:
            p = A_CORES.index(c)
            z = np.zeros((D, TP), np.float32)
            xt = [z, z, x_prompt[p, 0:TP].T, x_prompt[p, TP:2 * TP].T]
            m["keep"] = np.full((128, 1), 0.0, np.float32)
        m["xpT"] = np.ascontiguousarray(np.stack(xt))
        m["xsT"] = np.ascontiguousarray(x_sample[sl].reshape(TS, D).T)
        m["sgla"] = np.ascontiguousarray(state_gla[:, sl])
        m["sconvT"] = np.ascontiguousarray(state_conv[:, sl].transpose(0, 3, 1, 2).reshape(2, D, 2 * NSEQ_S))
        in_maps.append(m)
    res = run_bass_kernel_spmd(nc, in_maps, core_ids=list(range(NCORES)))
    R = res.results
    y_prompt = np.stack([np.concatenate([R[A_CORES[p]]["yT"].T, R[B_CORES[p]]["yT"].T], 0) for p in range(4)])
    y_sample = np.concatenate([R[c]["ysT"].T.reshape(NSEQ_S, 4, D) for c in range(NCORES)], 0)
    gla_p = np.stack([R[B_CORES[p]]["gp"] for p in range(4)], 1)
    gla_s = np.concatenate([R[c]["gs"] for c in range(NCORES)], 1)
    conv_p = np.stack([R[B_CORES[p]]["cpT"].transpose(0, 2, 1) for p in range(4)], 1)
    conv_s = np.concatenate([R[c]["csT"].reshape(2, D, NSEQ_S, 2).transpose(0, 2, 3, 1) for c in range(NCORES)], 1)
    cv = np.concatenate([R[c]["cv"].reshape(2, NSEQ_S, 4, 1024) for c in range(NCORES)], 1)
    out = (y_prompt, y_sample, gla_p, gla_s, conv_p, conv_s, cv)
    return tuple(np.ascontiguousarray(o, dtype=np.float32) for o in out)
```

```python
import contextlib
import numpy as np
import concourse.bass as bass
import concourse.mybir as mybir
from concourse.bass_utils import run_bass_kernel_spmd

F32 = mybir.dt.float32
BF16 = mybir.dt.bfloat16
AF = mybir.ActivationFunctionType
ALU = mybir.AluOpType

NCORES = 8
D = 2048
KC = 16
TP = 512
TS = 64
TT = TP + TS
NPASS = 4
SEQ = 2048
NSEQ_S = 16
DEPTH = 4
EVEN_IN = 5136
DFF = 8192
EPS = 1e-6
WCOLS = 256
NSLOT = 4
SAME_SYNC = True
B_CORES = [0, 1, 4, 5]
A_CORES = [2, 3, 6, 7]

C_ID = 0
C_M128 = 128
C_M64 = 256
C_MS = 384
C_OH = 448
C_RM = 464
NCONST = C_RM + TT


class Sched:
    ENGS = ("pe", "act", "dve", "pool", "sp")

    def __init__(self):
        self.ops = {e: [] for e in self.ENGS}
        self.cnt = {}
        self.res = {}
        self.known = {e: {} for e in self.ENGS}
        self.epoch = 0
        self.dma_cnt = {}
        self.waited = {}
        self.final_waits = {}
        self.dry = False

    def new_epoch(self):
        self.epoch += 1

    def _collect(self, eng, rd, wr):
        need = {}

        def add(evt):
            if evt is None:
                return
            k, idx = evt
            if need.get(k, 0) < idx:
                need[k] = idx
        for r in rd:
            st = self.res.get(r)
            if st:
                add(st[0])
        for w in wr:
            st = self.res.get(w)
            if st:
                add(st[0])
                for k, idx in st[1].items():
                    add((k, idx))
        out = []
        for k, idx in need.items():
            if k[0] == "eng" and k[1] == eng:
                if eng in ("pe", "pool", "sp"):
                    continue
                if not SAME_SYNC:
                    continue
            if self.known[eng].get(k, 0) >= idx:
                continue
            self.known[eng][k] = idx
            out.append((k, idx))
            if k[0] == "eng":
                self.waited.setdefault(k, set()).add(idx)
        return out

    def _update(self, evt, rd, wr):
        k, idx = evt
        for r in rd:
            st = self.res.setdefault(r, [None, {}])
            if st[1].get(k, 0) < idx:
                st[1][k] = idx
        for w in wr:
            self.res[w] = [evt, {}]

    def op(self, eng, fn, rd=(), wr=()):
        if self.dry:
            return
        waits = self._collect(eng, rd, wr)
        k = ("eng", eng, self.epoch)
        n = self.cnt.get(k, 0) + 1
        self.cnt[k] = n
        self.ops[eng].append([fn, waits, (k, n), 1])
        self._update((k, n), rd, wr)

    def fence(self, eng, res):
        if self.dry:
            return
        waits = self._collect(eng, (), res)
        if waits:
            self.ops[eng].append([None, waits, None, 0])

    def dma(self, q, fn, key, rd=(), wr=(), final=False):
        if self.dry:
            return
        waits = self._collect(q, rd, wr)
        k = ("dma", key)
        n = self.dma_cnt.get(k, 0) + 1
        self.dma_cnt[k] = n
        self.ops[q].append([fn, waits, (k, n), 16])
        self._update((k, n), rd, wr)
        if final:
            self.final_waits[k] = n

    def barrier(self, engs=("pe", "act", "dve")):
        if self.dry:
            return
        for e in engs:
            waits = []
            for o in engs:
                if o == e:
                    continue
                for ep in range(self.epoch + 1):
                    k = ("eng", o, ep)
                    n = self.cnt.get(k, 0)
                    if n and self.known[e].get(k, 0) < n:
                        self.known[e][k] = n
                        waits.append((k, n))
                        self.waited.setdefault(k, set()).add(n)
            if waits:
                self.ops[e].append([None, waits, None, 0])

    def emit(self, nc, stack):
        sems = {}
        rank = {}
        for k, s in self.waited.items():
            rank[k] = {idx: i + 1 for i, idx in enumerate(sorted(s))}

        def sem(k):
            if k not in sems:
                sems[k] = stack.enter_context(nc.semaphore("s%d" % len(sems)))
            return sems[k]

        def val(k, idx):
            return 16 * idx if k[0] == "dma" else rank[k][idx]
        fw = [(k, n) for k, n in self.final_waits.items()]
        self.ops["sp"].append([None, fw, None, 0])
        block = stack.enter_context(nc.Block())
        getters = {"pe": block.tensor, "act": block.scalar, "dve": block.vector,
                   "pool": block.gpsimd, "sp": block.sync}
        for e in self.ENGS:
            ops = self.ops[e]

            def body(eng, ops=ops):
                for fn, waits, evt, inc in ops:
                    for k, idx in waits:
                        eng.wait_ge(sem(k), val(k, idx))
                    if fn is None:
                        continue
                    ins = fn(eng)
                    if inc == 16:
                        ins.then_inc(sem(evt[0]), 16)
                    elif evt[1] in rank.get(evt[0], ()):
                        ins.then_inc(sem(evt[0]), 1)
            getters[e](body)


def build_program():
    nc = bass.Bass("TRN2", target_bir_lowering=False)
    S = Sched()
    stack = contextlib.ExitStack()

    def din(name, shape):
        return nc.dram_tensor(name, list(shape), F32, kind="ExternalInput").ap()

    def dout(name, shape):
        return nc.dram_tensor(name, list(shape), F32, kind="ExternalOutput").ap()

    xpT = din("xpT", [NPASS, D, TP])
    keep_d = din("keep", [128, 1])
    xsT = din("xsT", [D, TS])
    sgla = din("sgla", [2, NSEQ_S, 4, 128, 256])
    sconvT = din("sconvT", [2, D, 2 * NSEQ_S])
    w_in_even = din("w_in_even", [2, D, EVEN_IN])
    w_out_even = din("w_out_even", [2, D, D])
    w_in_odd = din("w_in_odd", [2, D, 3 * D])
    w_out_odd = din("w_out_odd", [2, D, D])
    w_ffn_up = din("w_ffn_up", [4, D, DFF])
    w_ffn_down = din("w_ffn_down", [4, DFF, D])
    w_gate_up = din("w_gate_up", [16, 2, 512])
    gains_d = din("gains", [128, 9, 16])
    ggla_d = din("ggla", [128, 2, 8])
    convw_d = din("convw", [128, 2, 3, 16])
    bgate_d = din("bgate", [128, 2, 4])
    wspT_d = din("wspT", [128, 2, 4, 128])
    wsps_d = din("wsps", [64, 2, 4, 64])
    bsp_d = din("bsp", [128, 2, 4, 128])
    bsps_d = din("bsps", [128, 2, 4, 64])
    consts_d = din("consts", [128, NCONST])

    yT = dout("yT", [D, 2 * TP])
    ysT = dout("ysT", [D, TS])
    gp = dout("gp", [2, 4, 128, 256])
    gs = dout("gs", [2, NSEQ_S, 4, 128, 256])
    cpT = dout("cpT", [2, D, 2])
    csT = dout("csT", [2, D, 2 * NSEQ_S])
    cv = dout("cv", [2, TS, 1024])

    def sb(name, shape, dt):
        return stack.enter_context(nc.sbuf_tensor(name, list(shape), dt))

    xT = sb("xT", [128, KC, TT], F32)
    hT = sb("hT", [128, KC * TT], BF16)
    wsl = sb("wsl", [128, NSLOT, KC, WCOLS], BF16)
    S32 = sb("S32", [128, 2, 4, 256], F32)
    Sbf = sb("Sbf", [128, 2, 4, 256], BF16)
    convh = sb("convh", [128, 2, KC, 2], F32)
    rstd = sb("rstd", [128, TT], F32)
    sqb = sb("sqb", [128, 2, TT], BF16)
    cst = sb("cst", [128, NCONST], F32)
    identb = sb("identb", [128, 128], BF16)
    onesb = sb("onesb", [128, 128], BF16)
    gains = sb("gains_s", [128, 9, 16], F32)
    ggla = sb("ggla_s", [128, 2, 8], F32)
    convw = sb("convw_s", [128, 2, 3, 16], F32)
    negb = sb("negb", [128, 2, 4], F32)
    wg = sb("wg", [16, 2, 512], BF16)
    wTm = sb("wTm", [128, 2, 4, 128], BF16)
    Wbd = sb("Wbd", [64, 2, 4, 64], BF16)
    bsp = sb("bsp_s", [128, 2, 4, 128], F32)
    bsps = sb("bsps_s", [128, 2, 4, 64], F32)
    v32 = sb("v32", [64, 1024], F32)
    Sj32 = sb("Sj32", [128, 2, 4, 256], F32)
    eGl = sb("eGl", [128, 4, 8 + NSEQ_S], F32)
    cstage = sb("cstage", [128, KC, NSEQ_S, 2], F32)
    epsb = sb("epsb", [128, 1], F32)
    keep = sb("keep_s", [128, 1], F32)
    xsave = sb("xsave", [128, KC, TS], F32)
    catsave = sb("catsave", [128, KC, TS], BF16)
    WORK_BYTES = 68 * 1024
    work = sb("work", [128, WORK_BYTES // 4], F32)

    ps = stack.enter_context(nc.psum_tensor("ps", [128, 8, 512], F32))

    ident32 = cst[:, C_ID:C_ID + 128]
    m128 = cst[:, C_M128:C_M128 + 128]
    m64 = cst[:, C_M64:C_M64 + 128]
    ms = cst[0:64, C_MS:C_MS + 64]
    onehot = cst[0:64, C_OH:C_OH + 16]
    rmask = cst[:, C_RM:C_RM + TT]

    def carve(off_bytes, n, dt):
        if dt == F32:
            assert off_bytes % 4 == 0
            a = work[:, off_bytes // 4: off_bytes // 4 + n]
            return a
        assert off_bytes % 4 == 0 and n % 2 == 0
        a = work[:, off_bytes // 4: off_bytes // 4 + n // 2]
        return a.bitcast(BF16)

    def hcarve(off_bytes, n, dt):
        if dt == BF16:
            return hT[:, off_bytes // 2: off_bytes // 2 + n]
        a = hT[:, off_bytes // 2: off_bytes // 2 + 2 * n]
        return a.bitcast(F32)

    pool_state = {"big": 0, "small": 0, "pinned": set()}

    def alloc_big():
        while True:
            b = pool_state["big"] % 8
            pool_state["big"] += 1
            if b not in pool_state["pinned"]:
                break
        return ps[:, b, :], ("psb", b)

    def alloc_for(n):
        return alloc_big()

    plan = []
    wstate = {"next": 0, "issued": 0}

    def slot_view(s, kcs, ncols):
        if kcs == KC:
            return wsl[:, s, :, 0:ncols]
        return wsl[:, s].rearrange("p k n -> p (k n)").rearrange("p (k n) -> p k n", n=ncols)

    def w_issue(j):
        ap2d, ncols, kcs = plan[j]
        s = j % NSLOT
        src = ap2d.rearrange("(kc p) n -> p kc n", p=128)
        dst = slot_view(s, kcs, ncols)
        S.dma("pool", lambda e, dst=dst, src=src: e.dma_start(out=dst, in_=src),
              key=("w", s), wr=[("wsl", s)])

    def w_next(ap2d, ncols, kcs=KC):
        if S.dry:
            plan.append((ap2d, ncols, kcs))
            return slot_view(0, kcs, ncols), ("wsl", 0)
        i = wstate["next"]
        wstate["next"] += 1
        while wstate["issued"] < min(len(plan), i + NSLOT):
            w_issue(wstate["issued"])
            wstate["issued"] += 1
        s = i % NSLOT
        return slot_view(s, kcs, ncols), ("wsl", s)

    evac_rr = {"i": 0}

    def rr_eng():
        evac_rr["i"] += 1
        return "act" if evac_rr["i"] % 2 else "dve"

    def copy_op(eng, out, in_, rd, wr):
        if eng == "act":
            S.op("act", lambda e: e.activation(out=out, in_=in_, func=AF.Copy), rd=rd, wr=wr)
        else:
            S.op("dve", lambda e: e.tensor_copy(out=out, in_=in_), rd=rd, wr=wr)

    def gemm_fm(tiles, chunk_list, rhs_fn):
        for spec in chunk_list:
            kq_srcs, ncols, evac = spec[:3]
            kcs = spec[3] if len(spec) > 3 else KC
            nnb = max(1, ncols // 128)
            mcols = min(ncols, 128)
            acc = {}
            shared = None
            for nb in range(nnb):
                for ti, (c0, n) in enumerate(tiles):
                    if n <= 64 and nnb == 4:
                        if shared is None:
                            shared = alloc_big()
                        acc[(nb, ti)] = (shared[0][:, 64 * nb:64 * (nb + 1)], shared[1], True)
                    else:
                        pa, pr = alloc_big()
                        acc[(nb, ti)] = (pa, pr, False)
            nkq = len(kq_srcs)
            for kq, src in enumerate(kq_srcs):
                slot, sres = w_next(src, ncols, kcs)
                for nb in range(nnb):
                    for ti, (c0, n) in enumerate(tiles):
                        pa, pr, sh = acc[(nb, ti)]
                        for kc in range(kcs):
                            rap, rres = rhs_fn(kq * kcs + kc, ti, c0, n)
                            first = (kq == 0 and kc == 0)
                            last = (kq == nkq - 1 and kc == kcs - 1)
                            if sh:
                                S.op("pe", lambda e, pa=pa, slot=slot, kc=kc, nb=nb, rap=rap, n=n, st=(first and nb == 0):
                                     e.matmul(pa[:, 0:n], lhsT=slot[:, kc, nb * 128: nb * 128 + 128], rhs=rap,
                                              start=st, stop=False, skip_group_check=True),
                                     rd=[sres, rres], wr=[pr])
                            else:
                                S.op("pe", lambda e, pa=pa, slot=slot, kc=kc, nb=nb, rap=rap, n=n, mcols=mcols, first=first, last=last:
                                     e.matmul(pa[0:mcols, 0:n], lhsT=slot[:, kc, nb * 128: nb * 128 + mcols], rhs=rap,
                                              start=first, stop=last),
                                     rd=[sres, rres], wr=[pr])
            for nb in range(nnb):
                for ti, (c0, n) in enumerate(tiles):
                    pa, pr, sh = acc[(nb, ti)]
                    evac(nb, ti, pa, pr, c0, n)

    def wide(W2d, r0, nrows, c0):
        return [W2d[r0 + 1024 * q: r0 + 1024 * (q + 1), c0:c0 + 512] for q in range(nrows // 1024)]

    def rmsnorm(tiles, gidx, out_fn):
        for ti, (c0, n) in enumerate(tiles):
            pa, pr = alloc_for(n)
            for kc in range(KC):
                sq = sqb[:, kc % 2, 0:n]
                S.op("act", lambda e, sq=sq, kc=kc, c0=c0, n=n: e.activation(out=sq, in_=xT[:, kc, c0:c0 + n], func=AF.Square),
                     rd=[("xT", kc, ti)], wr=[("sqb", kc % 2)])
                S.op("pe", lambda e, pa=pa, sq=sq, kc=kc, n=n: e.matmul(pa[:, 0:n], lhsT=onesb[:, :], rhs=sq,
                                                                       start=(kc == 0), stop=(kc == KC - 1)),
                     rd=[("sqb", kc % 2), "onesb"], wr=[pr])
            S.op("act", lambda e, pa=pa, c0=c0, n=n: e.activation(out=rstd[:, c0:c0 + n], in_=pa[:, 0:n], func=AF.Sqrt,
                                                                 scale=1.0 / D, bias=epsb[:, 0:1]),
                 rd=[pr, "epsb"], wr=[("rstd", ti)])
            S.op("dve", lambda e, c0=c0, n=n: e.reciprocal(out=rstd[:, c0:c0 + n], in_=rstd[:, c0:c0 + n]),
                 rd=[("rstd", ti)], wr=[("rstd", ti)])
            for kc in range(KC):
                oap, ores = out_fn(kc, ti, c0, n)
                S.op("dve", lambda e, oap=oap, kc=kc, c0=c0, n=n: e.scalar_tensor_tensor(
                    out=oap, in0=xT[:, kc, c0:c0 + n], scalar=gains[:, gidx, kc:kc + 1], in1=rstd[:, c0:c0 + n],
                    op0=ALU.mult, op1=ALU.mult), rd=[("xT", kc, ti), ("rstd", ti), "gains"], wr=[ores])

    hT3 = hT[:, :].rearrange("p (k t) -> p k t", t=TT)

    def h_out(kc, ti, c0, n):
        return hT3[:, kc, c0:c0 + n], ("hT", kc, ti)

    def h_rhs(kc, ti, c0, n):
        return hT3[:, kc, c0:c0 + n], ("hT", kc, ti)

    def add_to_x(nb_glob):
        def evac(nb, ti, pa, pr, c0, n, nb_glob=nb_glob):
            g = nb_glob + nb
            S.op("dve", lambda e: e.tensor_tensor(out=xT[:, g, c0:c0 + n], in0=pa[:, 0:n], in1=xT[:, g, c0:c0 + n], op=ALU.add),
                 rd=[pr, ("xT", g, ti)], wr=[("xT", g, ti)])
        return evac

    def setup():
        def ld(dst, src, res):
            S.dma("sp", lambda e: e.dma_start(out=dst, in_=src), key=("c", res), wr=[res])
        ld(cst[:, :], consts_d[:, :], "cst")
        ld(keep[:, :], keep_d[:, :], "keep")
        ld(gains[:, :, :], gains_d[:, :, :], "gains")
        ld(ggla[:, :, :], ggla_d[:, :, :], "ggla")
        ld(convw[:, :, :, :], convw_d[:, :, :, :], "convw")
        ld(negb[:, :, :], bgate_d[:, :, :], "negb")
        ld(bsp[:, :, :, :], bsp_d[:, :, :, :], "bsp")
        ld(bsps[:, :, :, :], bsps_d[:, :, :, :], "bsps")
        wsp32 = carve(0, 2 * 4 * 128, F32).rearrange("p (i h t) -> p i h t", i=2, h=4)
        wss32 = carve(4096, 2 * 4 * 64, F32)[0:64].rearrange("p (i h t) -> p i h t", i=2, h=4)
        ld(wsp32, wspT_d[:, :, :, :], "wsp32")
        ld(wss32, wsps_d[:, :, :, :], "wss32")
        S.dma("pool", lambda e: e.dma_start(out=identb[:, :], in_=consts_d[:, C_ID:C_ID + 128]), key=("c", "identb"), wr=["identb"])
        S.dma("pool", lambda e: e.dma_start(out=wg[:, :, :], in_=w_gate_up[:, :, :]), key=("c", "wg"), wr=["wg"])
        S.op("dve", lambda e: e.memset(onesb[:, :], 1.0), wr=["onesb"])
        S.op("dve", lambda e: e.memset(epsb[:, :], EPS), wr=["epsb"])
        S.op("dve", lambda e: e.memset(S32[:, :, :, :], 0.0), wr=[("S32", i, h) for i in range(2) for h in range(4)])
        S.op("dve", lambda e: e.memset(Sbf[:, :, :, :], 0.0), wr=[("Sbf", i, h) for i in range(2) for h in range(4)])
        S.op("dve", lambda e: e.memset(convh[:, :, :, :], 0.0), wr=[("convh", i, g) for i in range(2) for g in range(KC)])
        S.op("dve", lambda e: e.tensor_scalar(out=negb[:, :, :], in0=negb[:, :, :], scalar1=-1.0, scalar2=None, op0=ALU.mult),
             rd=["negb"], wr=["negb"])
        for i in range(2):
            for h in range(4):
                S.op("dve", lambda e, i=i, h=h: e.tensor_tensor(out=wTm[:, i, h, :], in0=wsp32[:, i, h, :], in1=m128, op=ALU.mult),
                     rd=["wsp32", "cst"], wr=["wTm"])
                S.op("dve", lambda e, i=i, h=h: e.tensor_tensor(out=Wbd[:, i, h, :], in0=wss32[:, i, h, :], in1=ms, op=ALU.mult),
                     rd=["wss32", "cst"], wr=["Wbd"])

    O_QIN = 0
    O_KIN = O_QIN + 4 * TT * 2
    O_KDT = O_KIN + 4 * TT * 2
    O_Q32 = O_KDT + 4 * TT * 2
    O_GLR = O_Q32 + 4 * TS * 4
    O_RT = O_GLR + TT * 2
    O_R2 = O_RT + 8 * TT * 2
    O_CS = O_R2
    O_EG = O_CS + 4 * TT * 4
    O_EGN = O_EG + 4 * TT * 4
    O_EGD = O_EGN + 4 * TT * 4
    O_E1 = O_EGD + 4 * TT * 4
    O_SP = O_E1 + TT * 4
    O_DD = O_SP + TT * 4
    END_P1 = O_DD + TT * 4
    O_CAT = O_R2
    O_VB = O_CAT + 16 * TT * 2
    O_VT = O_VB + 5 * 1024 * 2
    O_T1 = O_VT + 5 * 1024 * 2
    END_PA = O_T1 + 2 * 512 * 4
    O_KDTOK = O_VT
    O_KDTS = O_KDTOK + 4 * 512 * 2
    O_KDM = O_KDTS + 512 * 2
    END_PB = O_KDM + 16 * 128 * 2
    assert END_PB <= O_T1, (END_PB, O_T1)
    assert max(END_P1, END_PA) <= WORK_BYTES, (END_P1, END_PA, WORK_BYTES)
    H_OT = 0
    H_OSQ = H_OT + 2 * TT * 4
    H_T2 = H_OSQ + 2 * TT * 2
    H_SCM = H_T2 + TT * 4
    H_SCMS = H_SCM + 2 * 128 * 2
    H_KDM2 = H_SCMS + 64 * 2
    H_KDM3 = H_KDM2 + 16 * 128 * 2
    assert H_KDM3 + 16 * 128 * 2 <= KC * TT * 2, (H_KDM3,)

    fence_res = []

    def even_mixer(i, tiles, pi, state_only=False, out_tiles=None, pre_out=None, save_tail=False):
        has_s = len(tiles) > 1
        Wm = w_in_even[i]
        q_in = carve(O_QIN, 4 * TT, BF16).rearrange("p (h t) -> p h t", h=4)
        k_in = carve(O_KIN, 4 * TT, BF16).rearrange("p (h t) -> p h t", h=4)
        kdT = carve(O_KDT, 4 * TT, BF16).rearrange("p (h t) -> p h t", h=4)
        q32 = carve(O_Q32, 4 * TS, F32).rearrange("p (h t) -> p h t", h=4)
        glrT = carve(O_GLR, TT, BF16)
        rT = carve(O_RT, 8 * TT, BF16).rearrange("p (c t) -> p c t", c=8)
        cs = carve(O_CS, 4 * TT, F32).rearrange("p (h t) -> p h t", h=4)
        eG = carve(O_EG, 4 * TT, F32).rearrange("p (h t) -> p h t", h=4)
        eGn = carve(O_EGN, 4 * TT, F32).rearrange("p (h t) -> p h t", h=4)
        egd = carve(O_EGD, 4 * TT, F32).rearrange("p (h t) -> p h t", h=4)
        e1 = carve(O_E1, TT, F32)
        spb = carve(O_SP, TT, F32)
        dd = carve(O_DD, TT, F32)
        catT = carve(O_CAT, 16 * TT, BF16).rearrange("p (c t) -> p c t", c=16)
        vb_tok = carve(O_VB, 5 * 1024, BF16).rearrange("p (j n) -> p j n", j=5)
        v_tok = carve(O_VT, 5 * 1024, BF16).rearrange("p (j n) -> p j n", j=5)
        t1 = carve(O_T1, 2 * 512, F32).rearrange("p (b t) -> p b t", b=2)
        kd_tok = carve(O_KDTOK, 4 * 512, BF16).rearrange("p (j n) -> p j n", j=4)
        kdts = carve(O_KDTS, 512, BF16)
        kdm_bufs = [carve(O_KDM, 16 * 128, BF16).rearrange("p (j d) -> p j d", j=16),
                    carve(O_T1, 16 * 128, BF16).rearrange("p (j d) -> p j d", j=16),
                    hcarve(H_KDM2, 16 * 128, BF16).rearrange("p (j d) -> p j d", j=16),
                    hcarve(H_KDM3, 16 * 128, BF16).rearrange("p (j d) -> p j d", j=16)]
        oT = hcarve(H_OT, 2 * TT, F32).rearrange("p (c t) -> p c t", c=2)
        osq = hcarve(H_OSQ, 2 * TT, BF16).rearrange("p (c t) -> p c t", c=2)
        t2 = hcarve(H_T2, TT, F32)
        scm = hcarve(H_SCM, 2 * 128, BF16).rearrange("p (b t) -> p b t", b=2)
        scms = hcarve(H_SCMS, 64, BF16)

        if fence_res:
            for eng in ("act", "dve"):
                S.fence(eng, list(fence_res))
            del fence_res[:]
        S.barrier()
        def ev_glr(nb, ti, pa, pr, c0, n):
            S.op("act", lambda e: e.activation(out=glrT[0:16, c0:c0 + n], in_=pa[0:16, 0:n], func=AF.Copy),
                 rd=[pr], wr=[("glrT", ti)])
        gemm_fm(tiles, [([Wm[:, 5120:5136]], 16, ev_glr)], h_rhs)
        for ti, (c0, n) in enumerate(tiles):
            L = 64 if ti == 0 else 4
            nch = n // L
            for h in range(4):
                pa, pr = alloc_for(n)
                S.op("pe", lambda e, pa=pa, h=h, c0=c0, n=n: e.matmul(pa[:, 0:n], lhsT=wg[0:16, i, h * 128:(h + 1) * 128],
                                                                     rhs=glrT[0:16, c0:c0 + n], start=True, stop=True),
                     rd=[("glrT", ti), "wg"], wr=[pr])
                S.op("act", lambda e, pa=pa, h=h, n=n: e.activation(out=e1[:, 0:n], in_=pa[:, 0:n], func=AF.Exp, scale=-1.0,
                                                                   bias=negb[:, i, h:h + 1]), rd=[pr, "negb"], wr=["e1"])
                S.op("act", lambda e, n=n: e.activation(out=spb[:, 0:n], in_=e1[:, 0:n], func=AF.Ln, scale=1.0, bias=1.0),
                     rd=["e1"], wr=["spb"])
                S.op("dve", lambda e, h=h, c0=c0, n=n: e.tensor_tensor_scan(out=cs[:, h, c0:c0 + n], data0=rmask[:, c0:c0 + n],
                                                                           data1=spb[:, 0:n], initial=0.0, op0=ALU.mult, op1=ALU.add),
                     rd=["spb", "cst"], wr=[("cs", h, ti)])
                S.op("act", lambda e, h=h, c0=c0, n=n: e.activation(out=eG[:, h, c0:c0 + n], in_=cs[:, h, c0:c0 + n], func=AF.Exp,
                                                                   scale=-1.0 / 16), rd=[("cs", h, ti)], wr=[("eG", h, ti)])
                S.op("act", lambda e, h=h, c0=c0, n=n: e.activation(out=eGn[:, h, c0:c0 + n], in_=cs[:, h, c0:c0 + n], func=AF.Exp,
                                                                   scale=1.0 / 16), rd=[("cs", h, ti)], wr=[("eGn", h, ti)])
                csv = cs[:, h, c0:c0 + n].rearrange("p (c t) -> p c t", t=L)
                S.op("dve", lambda e, csv=csv, n=n, L=L, nch=nch: e.tensor_tensor(
                    out=dd[:, 0:n].rearrange("p (c t) -> p c t", t=L), in0=csv[:, :, L - 1:L].broadcast_to([128, nch, L]),
                    in1=csv, op=ALU.subtract), rd=[("cs", h, ti)], wr=["dd"])
                S.op("act", lambda e, h=h, c0=c0, n=n: e.activation(out=egd[:, h, c0:c0 + n], in_=dd[:, 0:n], func=AF.Exp,
                                                                   scale=-1.0 / 16), rd=["dd"], wr=[("egd", h, ti)])
                eo = 0 if ti == 0 else 8
                egv = eG[:, h, c0:c0 + n].rearrange("p (c t) -> p c t", t=L)
                S.op("dve", lambda e, h=h, egv=egv, eo=eo, nch=nch, L=L: e.tensor_copy(
                    out=eGl[:, h, eo:eo + nch].unsqueeze(2), in_=egv[:, :, L - 1:L]), rd=[("eG", h, ti)], wr=[("eGl", h, ti)])
        def ev_q(base):
            def ev(nb, ti, pa, pr, c0, n):
                h = base + nb
                S.op("dve", lambda e: e.scalar_tensor_tensor(out=q_in[:, h, c0:c0 + n], in0=pa[:, 0:n], scalar=float(128 ** -0.5),
                                                             in1=eG[:, h, c0:c0 + n], op0=ALU.mult, op1=ALU.mult),
                     rd=[pr, ("eG", h, ti)], wr=[("q_in", h, ti)])
                if ti == 1:
                    S.op("dve", lambda e: e.scalar_tensor_tensor(out=q32[:, h, 0:n], in0=pa[:, 0:n], scalar=float(128 ** -0.5),
                                                                 in1=eG[:, h, c0:c0 + n], op0=ALU.mult, op1=ALU.mult),
                         rd=[pr, ("eG", h, ti)], wr=[("q32", h)])
            return ev

        def ev_k(base):
            def ev(nb, ti, pa, pr, c0, n):
                h = base + nb
                if not state_only:
                    S.op("dve", lambda e: e.tensor_tensor(out=k_in[:, h, c0:c0 + n], in0=pa[:, 0:n], in1=eGn[:, h, c0:c0 + n], op=ALU.mult),
                         rd=[pr, ("eGn", h, ti)], wr=[("k_in", h, ti)])
                S.op("dve", lambda e: e.tensor_tensor(out=kdT[:, h, c0:c0 + n], in0=pa[:, 0:n], in1=egd[:, h, c0:c0 + n], op=ALU.mult),
                     rd=[pr, ("egd", h, ti)], wr=[("kdT", h, ti)])
            return ev

        def ev_act(dst, base, func, name):
            def ev(nb, ti, pa, pr, c0, n):
                c = base + nb
                S.op("act", lambda e: e.activation(out=dst[:, c, c0:c0 + n], in_=pa[:, 0:n], func=func),
                     rd=[pr], wr=[(name, c, ti)])
            return ev
        chunks = []
        if not state_only:
            chunks.append((wide(Wm, 0, D, 2048), 512, ev_q(0), 8))
        chunks.append((wide(Wm, 0, D, 2560), 512, ev_k(0), 8))
        gemm_fm(tiles, chunks, h_rhs)
        S.barrier(("act", "dve"))
        chunks = []
        for b in range(0 if state_only else 2):
            chunks.append((wide(Wm, 0, D, 4096 + 512 * b), 512, ev_act(rT, 4 * b, AF.Silu, "rT"), 8))
        for b in range(0 if state_only else 2):
            chunks.append((wide(Wm, 0, D, 512 * b), 512, ev_act(catT, 4 * b, AF.Gelu_apprx_tanh, "cat"), 8))
        gemm_fm(tiles, chunks, h_rhs)
        subt = [(128 * j, 128) for j in range(4)] + ([(TP, TS)] if has_s else [])
        for which in ((1,) if state_only else (0, 1)):
            for b in range(4):
                col0 = (1024 if which == 0 else 3072) + 256 * b
                slot, sres = w_next(Wm[:, col0:col0 + 256], 256)
                for j, (c0, m) in enumerate(subt):
                    pa, pr = alloc_big()
                    for kc in range(KC):
                        S.op("pe", lambda e, pa=pa, slot=slot, kc=kc, c0=c0, m=m: e.matmul(
                            pa[0:m, 0:256], lhsT=hT3[:, kc, c0:c0 + m], rhs=slot[:, kc, 0:256], start=(kc == 0), stop=(kc == KC - 1)),
                            rd=[sres, ("hT", kc, 0 if j < 4 else 1)], wr=[pr])
                    if which == 0:
                        if j < 4:
                            S.op("act", lambda e, pa=pa, j=j, b=b, m=m: e.activation(out=v_tok[0:m, j, 256 * b:256 * (b + 1)], in_=pa[0:m, 0:256],
                                                                                    func=AF.Gelu_apprx_tanh), rd=[pr], wr=[("v_tok", j, b)])
                        else:
                            S.op("act", lambda e, pa=pa, b=b, m=m: e.activation(out=v32[0:m, 256 * b:256 * (b + 1)], in_=pa[0:m, 0:256],
                                                                               func=AF.Gelu_apprx_tanh), rd=[pr], wr=[("v32", b)])
                            S.op("dve", lambda e, b=b, m=m, j=j: e.tensor_copy(out=v_tok[0:m, j, 256 * b:256 * (b + 1)], in_=v32[0:m, 256 * b:256 * (b + 1)]),
                                 rd=[("v32", b)], wr=[("v_tok", j, b)])
                    else:
                        copy_op(rr_eng(), vb_tok[0:m, j, 256 * b:256 * (b + 1)], pa[0:m, 0:256], [pr], [("vb_tok", j, b)])
        if has_s:
            S.dma("sp", lambda e: e.dma_start(out=cv[i, :, :], in_=v32[:, :]), key=("cv",), rd=[("v32", b) for b in range(4)], final=True)
        for c in range(0 if state_only else 8):
            h = c // 2
            pa, pr = alloc_big()
            for j in range(4):
                S.op("pe", lambda e, pa=pa, j=j, c=c, h=h: e.matmul(pa[:, 128 * j:128 * (j + 1)], lhsT=v_tok[:, j, 128 * c:128 * (c + 1)],
                                                                   rhs=wTm[:, i, h, :], start=True, stop=True),
                     rd=[("v_tok", j, c // 2), "wTm"], wr=[pr])
            tb = t1[:, c % 2, :]
            S.op("dve", lambda e, pa=pa, tb=tb, h=h: e.tensor_tensor(
                out=tb.rearrange("p (j t) -> p j t", j=4), in0=pa[:, :].rearrange("p (j t) -> p j t", j=4),
                in1=bsp[:, i, h, :].unsqueeze(1).broadcast_to([128, 4, 128]), op=ALU.add), rd=[pr, "bsp"], wr=[("t1", c % 2)])
            S.op("dve", lambda e, tb=tb, c=c: e.tensor_tensor(out=catT[:, c, 0:TP], in0=tb, in1=catT[:, c, 0:TP], op=ALU.mult),
                 rd=[("t1", c % 2), ("cat", c, 0)], wr=[("cat", c, 0)])
            if has_s:
                pa2, pr2 = alloc_big()
                S.op("pe", lambda e, pa2=pa2, c=c, h=h: e.matmul(pa2[:, 0:TS], lhsT=v_tok[0:TS, 4, 128 * c:128 * (c + 1)],
                                                                rhs=Wbd[:, i, h, :], start=True, stop=True),
                     rd=[("v_tok", 4, c // 2), "Wbd"], wr=[pr2])
                S.op("dve", lambda e, pa2=pa2, tb=tb, h=h: e.tensor_tensor(out=tb[:, 0:TS], in0=pa2[:, 0:TS], in1=bsps[:, i, h, :], op=ALU.add),
                     rd=[pr2, "bsps"], wr=[("t1", c % 2)])
                S.op("dve", lambda e, tb=tb, c=c: e.tensor_tensor(out=catT[:, c, TP:TT], in0=tb[:, 0:TS], in1=catT[:, c, TP:TT], op=ALU.mult),
                     rd=[("t1", c % 2), ("cat", c, 1)], wr=[("cat", c, 1)])
        S.barrier()
        for j in range(4):
            pa, pr = alloc_big()
            pab = pa.bitcast(BF16)
            for h in range(4):
                S.op("pe", lambda e, pab=pab, h=h, j=j: e.transpose(pab[:, 128 * h:128 * (h + 1)], kdT[:, h, 128 * j:128 * (j + 1)], identb[:, :]),
                     rd=[("kdT", h, 0), "identb"], wr=[pr])
            copy_op(rr_eng(), kd_tok[:, j, :], pab[:, 0:512], [pr], [("kd_tok", j)])

        def norm_head(h, c0, n, ti):
            pn, prn = alloc_big()
            for dc in range(2):
                S.op("pe", lambda e, pn=pn, dc=dc, c0=c0, n=n: e.matmul(pn[:, 0:n], lhsT=onesb[:, :], rhs=osq[:, dc, c0:c0 + n],
                                                                       start=(dc == 0), stop=(dc == 1)),
                     rd=[("osq", dc, ti), "onesb"], wr=[prn])
            S.op("act", lambda e, pn=pn, c0=c0, n=n: e.activation(out=t2[:, c0:c0 + n], in_=pn[:, 0:n], func=AF.Sqrt, scale=1.0 / 256, bias=epsb[:, 0:1]),
                 rd=[prn, "epsb"], wr=[("t2", ti)])
            S.op("dve", lambda e, c0=c0, n=n: e.reciprocal(out=t2[:, c0:c0 + n], in_=t2[:, c0:c0 + n]), rd=[("t2", ti)], wr=[("t2", ti)])
            for dc in range(2):
                c = 2 * h + dc
                S.op("dve", lambda e, dc=dc, c=c, c0=c0, n=n: e.scalar_tensor_tensor(
                    out=oT[:, dc, c0:c0 + n], in0=oT[:, dc, c0:c0 + n], scalar=ggla[:, i, c:c + 1], in1=t2[:, c0:c0 + n],
                    op0=ALU.mult, op1=ALU.mult), rd=[("oT", dc, ti), ("t2", ti), "ggla"], wr=[("oT", dc, ti)])
                S.op("dve", lambda e, dc=dc, c=c, c0=c0, n=n: e.tensor_tensor(out=catT[:, 8 + c, c0:c0 + n], in0=oT[:, dc, c0:c0 + n],
                                                                             in1=rT[:, c, c0:c0 + n], op=ALU.mult),
                     rd=[("oT", dc, ti), ("rT", c, ti)], wr=[("cat", 8 + c, ti)])

        if state_only:
            for h in range(4):
                for j in range(4):
                    for cc in range(2):
                        pS, prS = alloc_big()
                        S.op("pe", lambda e, pS=pS, cc=cc, j=j, h=h: e.matmul(
                            pS[:, 0:256], lhsT=kd_tok[64 * cc:64 * (cc + 1), j, 128 * h:128 * (h + 1)],
                            rhs=vb_tok[64 * cc:64 * (cc + 1), j, 256 * h:256 * (h + 1)], start=True, stop=True),
                            rd=[("kd_tok", j), ("vb_tok", j, h)], wr=[prS])
                        ch = 2 * j + cc
                        S.op("dve", lambda e, pS=pS, h=h, ch=ch: e.scalar_tensor_tensor(
                            out=S32[:, i, h, :], in0=S32[:, i, h, :], scalar=eGl[:, h, ch:ch + 1], in1=pS[:, 0:256],
                            op0=ALU.mult, op1=ALU.add), rd=[prS, ("S32", i, h), ("eGl", h, 0)], wr=[("S32", i, h)])
                S.op("act", lambda e, h=h: e.activation(out=Sbf[:, i, h, :], in_=S32[:, i, h, :], func=AF.Copy),
                     rd=[("S32", i, h)], wr=[("Sbf", i, h)])
        for h in range(0 if state_only else 4):
            for j in range(4):
                p1, pr1 = alloc_big()
                S.op("pe", lambda e, p1=p1, h=h, j=j: e.matmul(p1[:, 0:128], lhsT=k_in[:, h, 128 * j:128 * (j + 1)],
                                                              rhs=q_in[:, h, 128 * j:128 * (j + 1)], start=True, stop=True),
                     rd=[("k_in", h, 0), ("q_in", h, 0)], wr=[pr1])
                sc = scm[:, j % 2, :]
                S.op("dve", lambda e, p1=p1, sc=sc: e.tensor_tensor(out=sc, in0=p1[:, 0:128], in1=m64, op=ALU.mult),
                     rd=[pr1, "cst"], wr=[("scm", j % 2)])
                for cc in range(2):
                    pS, prS = alloc_big()
                    S.op("pe", lambda e, pS=pS, cc=cc, j=j, h=h: e.matmul(
                        pS[:, 0:256], lhsT=kd_tok[64 * cc:64 * (cc + 1), j, 128 * h:128 * (h + 1)],
                        rhs=vb_tok[64 * cc:64 * (cc + 1), j, 256 * h:256 * (h + 1)], start=True, stop=True),
                        rd=[("kd_tok", j), ("vb_tok", j, h)], wr=[prS])
                    if cc == 0:
                        po = []
                        for dc in range(2):
                            c = 2 * h + dc
                            pq, prq = alloc_big()
                            po.append((pq, prq))
                            S.op("pe", lambda e, pq=pq, c=c, j=j, sc=sc: e.matmul(pq[:, 0:128], lhsT=vb_tok[:, j, 128 * c:128 * (c + 1)],
                                                                              rhs=sc, start=True, stop=False),
                                 rd=[("vb_tok", j, h), ("scm", j % 2)], wr=[prq])
                            S.op("pe", lambda e, pq=pq, dc=dc, h=h, j=j: e.matmul(pq[:, 0:64], lhsT=Sbf[:, i, h, 128 * dc:128 * (dc + 1)],
                                                                              rhs=q_in[:, h, 128 * j:128 * j + 64], start=False, stop=False),
                                 rd=[("Sbf", i, h), ("q_in", h, 0)], wr=[prq])
                    ch = 2 * j + cc
                    S.op("dve", lambda e, pS=pS, h=h, ch=ch: e.scalar_tensor_tensor(
                        out=S32[:, i, h, :], in0=S32[:, i, h, :], scalar=eGl[:, h, ch:ch + 1], in1=pS[:, 0:256],
                        op0=ALU.mult, op1=ALU.add), rd=[prS, ("S32", i, h), ("eGl", h, 0)], wr=[("S32", i, h)])
                    S.op("act", lambda e, h=h: e.activation(out=Sbf[:, i, h, :], in_=S32[:, i, h, :], func=AF.Copy),
                         rd=[("S32", i, h)], wr=[("Sbf", i, h)])
                    if cc == 0:
                        for dc in range(2):
                            pq, prq = po[dc]
                            S.op("pe", lambda e, pq=pq, dc=dc, h=h, j=j: e.matmul(pq[:, 64:128], lhsT=Sbf[:, i, h, 128 * dc:128 * (dc + 1)],
                                                                              rhs=q_in[:, h, 128 * j + 64:128 * (j + 1)], start=False, stop=True),
                                 rd=[("Sbf", i, h), ("q_in", h, 0)], wr=[prq])
                for dc in range(2):
                    pq, prq = po[dc]
                    S.op("act", lambda e, pq=pq, dc=dc, j=j: e.activation(out=oT[:, dc, 128 * j:128 * (j + 1)], in_=pq[:, 0:128], func=AF.Copy),
                         rd=[prq], wr=[("oT", dc, 0)])
                    S.op("act", lambda e, pq=pq, dc=dc, j=j: e.activation(out=osq[:, dc, 128 * j:128 * (j + 1)], in_=pq[:, 0:128], func=AF.Square),
                         rd=[prq], wr=[("osq", dc, 0)])
            norm_head(h, 0, TP, 0)
        if pi == NPASS - 1:
            S.dma("sp", lambda e: e.dma_start(out=gp[i].rearrange("h d e -> d h e"), in_=S32[:, i, :, :]), key=("gp", i),
                  rd=[("S32", i, h) for h in range(4)], final=True)
        if has_s:
            pa, pr = alloc_big()
            pab = pa.bitcast(BF16)
            for h in range(4):
                S.op("pe", lambda e, pab=pab, h=h: e.transpose(pab[0:TS, 128 * h:128 * (h + 1)], kdT[:, h, TP:TT], identb[:, :]),
                     rd=[("kdT", h, 1), "identb"], wr=[pr])
            copy_op("dve", kdts[0:TS, :], pab[0:TS, 0:512], [pr], ["kdts"])
            acc = {}
            accb, accr = alloc_big()
            pool_state["pinned"].add(accr[1])
            for h in range(4):
                p1, pr1 = alloc_big()
                S.op("pe", lambda e, p1=p1, h=h: e.matmul(p1[0:TS, 0:TS], lhsT=k_in[:, h, TP:TT], rhs=q_in[:, h, TP:TT], start=True, stop=True),
                     rd=[("k_in", h, 1), ("q_in", h, 1)], wr=[pr1])
                S.op("dve", lambda e, p1=p1: e.tensor_tensor(out=scms[0:TS, :], in0=p1[0:TS, 0:TS], in1=ms, op=ALU.mult),
                     rd=[pr1, "cst"], wr=["scms"])
                for dc in range(2):
                    c = 2 * h + dc
                    pq, prq = accb[:, 64 * c:64 * (c + 1)], accr
                    acc[(h, dc)] = (pq, prq)
                    S.op("pe", lambda e, pq=pq, c=c: e.matmul(pq[:, 0:TS], lhsT=vb_tok[0:TS, 4, 128 * c:128 * (c + 1)], rhs=scms[0:TS, :],
                                                             start=(c == 0), stop=False, skip_group_check=True),
                         rd=[("vb_tok", 4, h), "scms"], wr=[prq])
            def sj_load(jj):
                bsel = jj % 2
                S.dma("sp", lambda e, jj=jj, bsel=bsel: e.dma_start(out=Sj32[:, bsel, :, :], in_=sgla[i, jj].rearrange("h d e -> d h e")),
                      key=("sj", bsel), wr=[("Sj32", bsel)])
            sj_load(0)
            for jj in range(NSEQ_S):
                bsel = jj % 2
                if jj + 1 < NSEQ_S:
                    sj_load(jj + 1)
                for h in range(4):
                    for dc in range(2):
                        pq, prq = acc[(h, dc)]
                        S.op("pe", lambda e, pq=pq, h=h, dc=dc, jj=jj, bsel=bsel: e.matmul(
                            pq[:, 4 * jj:4 * jj + 4], lhsT=Sj32[:, bsel, h, 128 * dc:128 * (dc + 1)], rhs=q32[:, h, 4 * jj:4 * jj + 4],
                            start=False, stop=False, skip_group_check=True), rd=[("Sj32", bsel), ("q32", h)], wr=[prq])
                for h in range(4):
                    if jj == 0:
                        S.op("dve", lambda e, h=h: e.tensor_tensor(
                            out=kdm_bufs[h][0:TS], in0=kdts[0:TS, 128 * h:128 * (h + 1)].unsqueeze(1).broadcast_to([TS, NSEQ_S, 128]),
                            in1=onehot.unsqueeze(2).broadcast_to([TS, NSEQ_S, 128]), op=ALU.mult), rd=["kdts", "cst"], wr=[("kdm", h)])
                    pS, prS = alloc_big()
                    S.op("pe", lambda e, pS=pS, h=h, jj=jj: e.matmul(pS[:, 0:256], lhsT=kdm_bufs[h][0:TS, jj, :], rhs=vb_tok[0:TS, 4, 256 * h:256 * (h + 1)],
                                                                    start=True, stop=True), rd=[("kdm", h), ("vb_tok", 4, h)], wr=[prS])
                    S.op("dve", lambda e, pS=pS, h=h, jj=jj, bsel=bsel: e.scalar_tensor_tensor(
                        out=Sj32[:, bsel, h, :], in0=Sj32[:, bsel, h, :], scalar=eGl[:, h, 8 + jj:8 + jj + 1], in1=pS[:, 0:256],
                        op0=ALU.mult, op1=ALU.add), rd=[prS, ("Sj32", bsel), ("eGl", h, 1)], wr=[("Sj32", bsel)])
                S.dma("sp", lambda e, jj=jj, bsel=bsel: e.dma_start(out=gs[i, jj].rearrange("h d e -> d h e"), in_=Sj32[:, bsel, :, :]),
                      key=("sjo", bsel), rd=[("Sj32", bsel)], final=True)
            for h in range(4):
                for dc in range(2):
                    pq, prq = acc[(h, dc)]
                    S.op("act", lambda e, pq=pq, dc=dc: e.activation(out=oT[:, dc, TP:TT], in_=pq[:, 0:TS], func=AF.Copy),
                         rd=[prq], wr=[("oT", dc, 1)])
                    S.op("act", lambda e, pq=pq, dc=dc: e.activation(out=osq[:, dc, TP:TT], in_=pq[:, 0:TS], func=AF.Square),
                         rd=[prq], wr=[("osq", dc, 1)])
                norm_head(h, TP, TS, 1)
            pool_state["pinned"].discard(accr[1])
        def cat_rhs(kc, ti, c0, n):
            return catT[:, kc, c0:c0 + n], ("cat", kc, ti)
        Wo = w_out_even[i]
        if save_tail:
            S.op("dve", lambda e: e.tensor_copy(out=xsave[:, :, :], in_=xT[:, :, TP - TS:TP]), rd=[("xT", kc, 0) for kc in range(KC)], wr=["xsave"])
            S.op("act", lambda e: e.activation(out=catsave[:, :, :], in_=catT[:, :, TP - TS:TP], func=AF.Copy),
                 rd=[("cat", c, 0) for c in range(16)], wr=["catsave"])
        if pre_out is not None:
            S.op("dve", lambda e: e.tensor_scalar(out=catT[:, :, TP:TT], in0=catsave[:, :, :], scalar1=keep[:, 0:1], scalar2=None, op0=ALU.mult),
                 rd=["catsave", "keep"], wr=[("cat", c, 1) for c in range(16)])
            S.op("dve", lambda e: e.tensor_scalar(out=xT[:, :, TP:TT], in0=xsave[:, :, :], scalar1=keep[:, 0:1], scalar2=None, op0=ALU.mult),
                 rd=["xsave", "keep"], wr=[("xT", kc, 1) for kc in range(KC)])
        ot = tiles if out_tiles is None else out_tiles
        if not state_only and ot:
            gemm_fm(ot, [(wide(Wo, 0, D, 512 * b), 512, add_to_x(4 * b), 8) for b in range(4)], cat_rhs)
        S.barrier()

    O_YIN = 0
    O_ZP = O_YIN + 16 * TT * 2
    O_ZPS = O_ZP + 2 * 516 * 4
    O_CG = O_ZPS + 2 * NSEQ_S * 6 * 4
    O_CV = O_CG + 2 * TT * 4
    O_ZTL = O_CV + 2 * TT * 4
    END_ODD = O_ZTL + 2 * TS * 4
    assert END_ODD <= WORK_BYTES

    def conv_mixer(i, tiles, pi, hist_only=False, tail=False, out_tiles=None):
        has_s = len(tiles) > 1 and not tail
        ztl = carve(O_ZTL, 2 * TS, F32).rearrange("p (b t) -> p b t", b=2)
        stash = {}
        Wm = w_in_odd[i]
        yin = carve(O_YIN, 16 * TT, BF16).rearrange("p (c t) -> p c t", c=16)
        zp = carve(O_ZP, 2 * 516, F32).rearrange("p (b t) -> p b t", b=2)
        zps = carve(O_ZPS, 2 * NSEQ_S * 6, F32).rearrange("p (b s t) -> p b s t", b=2, s=NSEQ_S)
        cgb = carve(O_CG, 2 * TT, F32).rearrange("p (b t) -> p b t", b=2)
        cvb = carve(O_CV, 2 * TT, F32).rearrange("p (b t) -> p b t", b=2)
        S.barrier()
        chunks = []
        for b in range(8):
            def ev_cg(nb, ti, pa, pr, c0, n, b=b):
                S.op("act", lambda e: e.activation(out=cgb[:, nb, c0:c0 + n], in_=pa[:, 0:n], func=AF.Copy), rd=[pr], wr=[("cgb", nb, ti)])

            def ev_hx(nb, ti, pa, pr, c0, n, b=b):
                g = 2 * b + nb
                if ti == 0 and hist_only:
                    S.op("dve", lambda e: e.tensor_tensor(out=zp[:, nb, 2 + c0:2 + c0 + n], in0=pa[:, 0:n], in1=cgb[:, nb, c0:c0 + n], op=ALU.mult),
                         rd=[pr, ("cgb", nb, 0)], wr=[("zp", nb)])
                    S.op("dve", lambda e: e.tensor_copy(out=convh[:, i, g, :], in_=zp[:, nb, TP:TP + 2]), rd=[("zp", nb)], wr=[("convh", i, g)])
                    return
                hist, hist_res = convh[:, i, g, :], ("convh", i, g)
                if tail and ti == 0:
                    stash[nb] = (pa, pr)
                    return
                if tail and ti == 1:
                    S.op("dve", lambda e, pat=pa: e.tensor_tensor(out=ztl[:, nb, 0:TS], in0=pat[:, 0:TS], in1=cgb[:, nb, TP:TT], op=ALU.mult),
                         rd=[pr, ("cgb", nb, 1)], wr=[("ztl", nb)])
                    pa, pr = stash[nb]
                    ti = 0
                    hist, hist_res = ztl[:, nb, TS - 2:TS], ("ztl", nb)
                if ti == 0:
                    S.op("dve", lambda e: e.tensor_copy(out=zp[:, nb, 0:2], in_=hist), rd=[hist_res], wr=[("zp", nb)])
                    S.op("dve", lambda e: e.tensor_tensor(out=zp[:, nb, 2:2 + TP], in0=pa[:, 0:TP], in1=cgb[:, nb, 0:TP], op=ALU.mult),
                         rd=[pr, ("cgb", nb, 0)], wr=[("zp", nb)])
                    S.op("dve", lambda e: e.tensor_copy(out=convh[:, i, g, :], in_=zp[:, nb, TP:TP + 2]), rd=[("zp", nb)], wr=[("convh", i, g)])
                    if hist_only:
                        return
                    S.op("dve", lambda e: e.tensor_scalar(out=cvb[:, nb, 0:TP], in0=zp[:, nb, 0:TP], scalar1=convw[:, i, 0, g:g + 1], scalar2=None,
                                                          op0=ALU.mult), rd=[("zp", nb), "convw"], wr=[("cvb", nb, 0)])
                    for t in (1, 2):
                        S.op("dve", lambda e, t=t: e.scalar_tensor_tensor(out=cvb[:, nb, 0:TP], in0=zp[:, nb, t:t + TP], scalar=convw[:, i, t, g:g + 1],
                                                                         in1=cvb[:, nb, 0:TP], op0=ALU.mult, op1=ALU.add),
                             rd=[("zp", nb), ("cvb", nb, 0)], wr=[("cvb", nb, 0)])
                else:
                    S.op("dve", lambda e: e.tensor_copy(out=zps[:, nb, :, 0:2], in_=cstage[:, g, :, :]), rd=[("cstage", g)], wr=[("zps", nb)])
                    S.op("dve", lambda e: e.tensor_tensor(out=zps[:, nb, :, 2:6], in0=pa[:, 0:TS].rearrange("p (s t) -> p s t", t=4),
                                                          in1=cgb[:, nb, TP:TT].rearrange("p (s t) -> p s t", t=4), op=ALU.mult),
                         rd=[pr, ("cgb", nb, 1)], wr=[("zps", nb)])
                    S.op("dve", lambda e: e.tensor_copy(out=cstage[:, g, :, :], in_=zps[:, nb, :, 4:6]), rd=[("zps", nb)], wr=[("cstage", g)])
                    cvs = cvb[:, nb, TP:TT].rearrange("p (s t) -> p s t", t=4)
                    S.op("dve", lambda e: e.tensor_scalar(out=cvs, in0=zps[:, nb, :, 0:4], scalar1=convw[:, i, 0, g:g + 1], scalar2=None,
                                                          op0=ALU.mult), rd=[("zps", nb), "convw"], wr=[("cvb", nb, 1)])
                    for t in (1, 2):
                        S.op("dve", lambda e, t=t: e.scalar_tensor_tensor(out=cvs, in0=zps[:, nb, :, t:t + 4], scalar=convw[:, i, t, g:g + 1],
                                                                         in1=cvs, op0=ALU.mult, op1=ALU.add),
                             rd=[("zps", nb), ("cvb", nb, 1)], wr=[("cvb", nb, 1)])

            def ev_bg(nb, ti, pa, pr, c0, n, b=b):
                g = 2 * b + nb
                if tail and ti == 1:
                    return
                S.op("dve", lambda e: e.tensor_tensor(out=yin[:, g, c0:c0 + n], in0=pa[:, 0:n], in1=cvb[:, nb, c0:c0 + n], op=ALU.mult),
                     rd=[pr, ("cvb", nb, ti)], wr=[("yin", g, ti)])
            chunks.append(([Wm[:, 2048 + 256 * b:2048 + 256 * (b + 1)]], 256, ev_cg))
            chunks.append(([Wm[:, 4096 + 256 * b:4096 + 256 * (b + 1)]], 256, ev_hx))
            if not hist_only:
                chunks.append(([Wm[:, 256 * b:256 * (b + 1)]], 256, ev_bg))
        if has_s:
            S.dma("sp", lambda e: e.dma_start(out=cstage[:, :, :, :].rearrange("p k s r -> p k (s r)"),
                                              in_=sconvT[i].rearrange("(k p) c -> p k c", p=128)),
                  key=("cst_in",), wr=[("cstage", g) for g in range(KC)])
        gemm_fm(tiles, chunks, h_rhs)
        if has_s:
            S.dma("sp", lambda e: e.dma_start(out=csT[i].rearrange("(k p) c -> p k c", p=128),
                                              in_=cstage[:, :, :, :].rearrange("p k s r -> p k (s r)")),
                  key=("cst_out",), rd=[("cstage", g) for g in range(KC)], final=True)
        if pi == NPASS - 1:
            S.dma("sp", lambda e: e.dma_start(out=cpT[i].rearrange("(k p) c -> p k c", p=128), in_=convh[:, i, :, :]),
                  key=("cp", i), rd=[("convh", i, g) for g in range(KC)], final=True)

        def yin_rhs(kc, ti, c0, n):
            return yin[:, kc, c0:c0 + n], ("yin", kc, ti)
        Wo = w_out_odd[i]
        if not hist_only:
            gemm_fm(out_tiles or tiles, [(wide(Wo, 0, D, 512 * b), 512, add_to_x(4 * b), 8) for b in range(4)], yin_rhs)
        S.barrier()

    O_HID = 0
    O_RL = O_HID + 32 * TT * 2
    assert O_RL + 4 * TT * 4 <= WORK_BYTES

    def ffn(l, tiles):
        hid = carve(O_HID, 32 * TT, BF16).rearrange("p (c t) -> p c t", c=32)
        rl = carve(O_RL, 4 * TT, F32).rearrange("p (b t) -> p b t", b=4)
        Wu = w_ffn_up[l]
        Wd = w_ffn_down[l]
        for half in range(2):
            chunks = []
            for b in range(8):
                def ev_up(nb, ti, pa, pr, c0, n, b=b):
                    c = 4 * b + nb
                    S.op("act", lambda e: e.activation(out=rl[:, nb, c0:c0 + n], in_=pa[:, 0:n], func=AF.Relu), rd=[pr], wr=[("rl", nb, ti)])
                    S.op("act", lambda e: e.activation(out=hid[:, c, c0:c0 + n], in_=rl[:, nb, c0:c0 + n], func=AF.Square),
                         rd=[("rl", nb, ti)], wr=[("hid", c, ti)])
                chunks.append((wide(Wu, 0, D, half * 4096 + 512 * b), 512, ev_up, 8))
            gemm_fm(tiles, chunks, h_rhs)

            def hid_rhs(kc, ti, c0, n):
                return hid[:, kc, c0:c0 + n], ("hid", kc, ti)
            chunks = []
            for b in range(4):
                chunks.append((wide(Wd, half * 4096, 4096, 512 * b), 512, add_to_x(4 * b), 8))
            gemm_fm(tiles, chunks, hid_rhs)

    SLOT_S = 3
    SLOT_T = 2
    XT = (TP, TS)

    def emit_pass(pi):
        mode = ("pre0", "pre1", "full", "full")[pi]
        tp = [(0, TP)]
        tiles = tp + ([XT] if pi == SLOT_S else [])
        both = tp + [XT]
        S.new_epoch()
        S.dma("sp", lambda e: e.dma_start(out=xT[:, :, 0:TP], in_=xpT[pi].rearrange("(k p) t -> p k t", p=128)),
              key=("xin",), wr=[("xT", kc, 0) for kc in range(KC)])
        if pi == SLOT_S:
            S.dma("sp", lambda e: e.dma_start(out=xT[:, :, TP:TT], in_=xsT[:, :].rearrange("(k p) t -> p k t", p=128)),
                  key=("xins",), wr=[("xT", kc, 1) for kc in range(KC)])
        if pi == SLOT_T:
            for i in range(2):
                for h in range(4):
                    S.op("dve", lambda e, i=i, h=h: e.tensor_scalar(out=S32[:, i, h, :], in0=S32[:, i, h, :], scalar1=keep[:, 0:1],
                                                                    scalar2=None, op0=ALU.mult), rd=[("S32", i, h), "keep"], wr=[("S32", i, h)])
                    S.op("act", lambda e, i=i, h=h: e.activation(out=Sbf[:, i, h, :], in_=S32[:, i, h, :], func=AF.Copy),
                         rd=[("S32", i, h)], wr=[("Sbf", i, h)])
                S.op("dve", lambda e, i=i: e.tensor_scalar(out=convh[:, i, :, :], in0=convh[:, i, :, :], scalar1=keep[:, 0:1],
                                                           scalar2=None, op0=ALU.mult),
                     rd=[("convh", i, g) for g in range(KC)] + ["keep"], wr=[("convh", i, g) for g in range(KC)])
        for l in range(DEPTH):
            i = l // 2
            if mode == "pre0" and l == 3:
                break
            tailslot = (pi == SLOT_T)
            lt = both if (tailslot and l == 3) else tiles
            ft = both if (tailslot and l == 2) else tiles
            rmsnorm(lt, l, h_out)
            if l % 2 == 0:
                so = (mode == "pre0" and l == 2)
                p1 = (mode == "pre1" and l == 2)
                even_mixer(i, tiles, pi, state_only=so,
                           out_tiles=([] if p1 else (both if (tailslot and l == 2) else None)),
                           pre_out=(True if (tailslot and l == 2) else None), save_tail=p1)
                if so or p1:
                    break
            else:
                conv_mixer(i, lt, pi, tail=(tailslot and l == 3), out_tiles=tiles)
            rmsnorm(ft, 4 + l, h_out)
            ffn(l, ft)
            if l == DEPTH - 1:
                S.barrier()
        if mode != "full":
            S.barrier()
            return
        ystage = carve(0, KC * TT, F32).rearrange("p (k t) -> p k t", k=KC)

        def y_out(kc, ti, c0, n):
            return ystage[:, kc, c0:c0 + n], ("ystage", kc, ti)
        rmsnorm(tiles, 8, y_out)
        yc = (pi - 2) * TP
        S.dma("sp", lambda e: e.dma_start(out=yT[:, yc:yc + TP].rearrange("(k p) t -> p k t", p=128), in_=ystage[:, :, 0:TP]),
              key=("yout",), rd=[("ystage", kc, 0) for kc in range(KC)], final=True)
        if pi == SLOT_S:
            S.dma("sp", lambda e: e.dma_start(out=ysT[:, :].rearrange("(k p) t -> p k t", p=128), in_=ystage[:, :, TP:TT]),
                  key=("youts",), rd=[("ystage", kc, 1) for kc in range(KC)], final=True)
        fence_res.extend(("ystage", kc, ti) for kc in range(KC) for ti in range(len(tiles)))
        S.barrier()

    def emit_all():
        setup()
        for pi in range(NPASS):
            emit_pass(pi)

    S.dry = True
    emit_all()
    S.dry = False
    S.epoch = 0
    del fence_res[:]
    pool_state["big"] = 0
    pool_state["pinned"] = set()
    evac_rr["i"] = 0
    emit_all()
    S.emit(nc, stack)
    stack.close()
    return nc


_NC = None


def _get_nc():
    global _NC
    if _NC is None:
        _NC = build_program()
    return _NC


def _consts():
    c = np.zeros((128, NCONST), np.float32)
    c[:, C_ID:C_ID + 128] = np.eye(128, dtype=np.float32)
    s = np.arange(128)[:, None]
    t = np.arange(128)[None, :]
    c[:, C_M128:C_M128 + 128] = (s <= t)
    c[:, C_M64:C_M64 + 128] = (s <= t) & (s // 64 == t // 64)
    s4 = np.arange(64)[:, None]
    t4 = np.arange(64)[None, :]
    c[0:64, C_MS:C_MS + 64] = (s4 <= t4) & (s4 // 4 == t4 // 4)
    c[0:64, C_OH:C_OH + 16] = (s4 // 4 == np.arange(16)[None, :])
    rm = np.ones(TT, np.float32)
    rm[0:TP:64] = 0.0
    rm[TP:TT:4] = 0.0
    c[:, C_RM:C_RM + TT] = rm[None, :]
    return c


def kernel(x_prompt, x_sample, state_gla, state_conv, norm_mix, norm_ffn, norm_final, w_in_even,
           w_gate_up, b_gate, w_spatial, b_spatial, g_gla_out, w_out_even, w_in_odd, conv_w,
           w_out_odd, w_ffn_up, w_ffn_down):
    f = lambda a: np.ascontiguousarray(np.asarray(a, dtype=np.float32))
    x_prompt, x_sample, state_gla, state_conv = f(x_prompt), f(x_sample), f(state_gla), f(state_conv)
    nc = _get_nc()

    def fm(a, n):
        return np.ascontiguousarray(a.reshape(n, 16, 128).transpose(2, 0, 1))
    gains = fm(np.concatenate([f(norm_mix), f(norm_ffn), f(norm_final)[None]], 0), 9)
    ggla = np.ascontiguousarray(f(g_gla_out).reshape(2, 8, 128).transpose(2, 0, 1))
    convw = np.ascontiguousarray(f(conv_w).reshape(2, 3, 16, 128).transpose(3, 0, 1, 2))
    bgate = np.ascontiguousarray(f(b_gate).reshape(2, 4, 128).transpose(2, 0, 1))
    wsp = f(w_spatial)
    wspT = np.ascontiguousarray(wsp.transpose(3, 0, 1, 2))
    w44 = wsp[:, :, 0:4, 0:4].transpose(3, 0, 1, 2)
    wsps = np.ascontiguousarray(np.tile(w44, (16, 1, 1, 16)))
    bs = f(b_spatial)
    bsp = np.ascontiguousarray(np.broadcast_to(bs[None], (128, 2, 4, 128)))
    bsps = np.ascontiguousarray(np.broadcast_to(np.tile(bs[:, :, 0:4], (1, 1, 16))[None], (128, 2, 4, 64)))
    wgu = np.ascontiguousarray(f(w_gate_up).transpose(1, 0, 2))
    consts = _consts()
    shared = dict(w_in_even=f(w_in_even), w_out_even=f(w_out_even), w_in_odd=f(w_in_odd), w_out_odd=f(w_out_odd),
                  w_ffn_up=f(w_ffn_up), w_ffn_down=f(w_ffn_down), w_gate_up=wgu, gains=gains, ggla=ggla, convw=convw,
                  bgate=bgate, wspT=wspT, wsps=wsps, bsp=bsp, bsps=bsps, consts=consts)
    in_maps = []
    for c in range(NCORES):
        sl = slice(NSEQ_S * c, NSEQ_S * (c + 1))
        m = dict(shared)
        if c in B_CORES:
            p = B_CORES.index(c)
            xt = [x_prompt[p, TP * t:TP * (t + 1)].T for t in range(4)]
            m["keep"] = np.full((128, 1), 1.0, np.float32)
        else:
            p = A_CORES.index(c)
            z = np.zeros((D, TP), np.float32)
            xt = [z, z, x_prompt[p, 0:TP].T, x_prompt[p, TP:2 * TP].T]
            m["keep"] = np.full((128, 1), 0.0, np.float32)
        m["xpT"] = np.ascontiguousarray(np.stack(xt))
        m["xsT"] = np.ascontiguousarray(x_sample[sl].reshape(TS, D).T)
        m["sgla"] = np.ascontiguousarray(state_gla[:, sl])
        m["sconvT"] = np.ascontiguousarray(state_conv[:, sl].transpose(0, 3, 1, 2).reshape(2, D, 2 * NSEQ_S))
        in_maps.append(m)
    res = run_bass_kernel_spmd(nc, in_maps, core_ids=list(range(NCORES)))
    R = res.results
    y_prompt = np.stack([np.concatenate([R[A_CORES[p]]["yT"].T, R[B_CORES[p]]["yT"].T], 0) for p in range(4)])
    y_sample = np.concatenate([R[c]["ysT"].T.reshape(NSEQ_S, 4, D) for c in range(NCORES)], 0)
    gla_p = np.stack([R[B_CORES[p]]["gp"] for p in range(4)], 1)
    gla_s = np.concatenate([R[c]["gs"] for c in range(NCORES)], 1)
    conv_p = np.stack([R[B_CORES[p]]["cpT"].transpose(0, 2, 1) for p in range(4)], 1)
    conv_s = np.concatenate([R[c]["csT"].reshape(2, D, NSEQ_S, 2).transpose(0, 2, 3, 1) for c in range(NCORES)], 1)
    cv = np.concatenate([R[c]["cv"].reshape(2, NSEQ_S, 4, 1024) for c in range(NCORES)], 1)
    out = (y_prompt, y_sample, gla_p, gla_s, conv_p, conv_s, cv)
    return tuple(np.ascontiguousarray(o, dtype=np.float32) for o in out)
```
